# Optimizing a Trainium2 kernel written in Bass

```python
import jax, jax.numpy as jnp
from jax import lax
import numpy as np

D_MODEL = 1024
BATCH = 1
SEQ = 16384
DEPTH = 4

GRID_W = 64
CTX_LEN = 256
NORM_EPS = 1e-6

RW_HEADS = 6
RW_HEAD_DIM = 64
RW_WIDTH = RW_HEADS * RW_HEAD_DIM
DECAY_LORA = 64
ICLR_LORA = 64
GATE_LORA = 128
RW_LN_EPS = 64e-5
MLA_HEADS = 6
Q_LORA = 384
KV_LORA = 256
NOPE_DIM = 64
ROPE_DIM = 32
V_DIM = 64
QK_DIM = NOPE_DIM + ROPE_DIM
MLA_WIDTH = MLA_HEADS * V_DIM
ROPE_BASE = 10000.0
Q_BLOCK = 128
FT_GROUPS = 4
FT_GROUP_DIM = 64
FT_WIDTH = FT_GROUPS * FT_GROUP_DIM
D_FF = 2816
N_BRANCH = 3
N_MOD = 6

RW_COLS = 3 * RW_WIDTH + 2 * DECAY_LORA + 2 * ICLR_LORA + GATE_LORA
RW_SPLITS = (RW_WIDTH, 2 * RW_WIDTH, 3 * RW_WIDTH,
             3 * RW_WIDTH + DECAY_LORA, 3 * RW_WIDTH + 2 * DECAY_LORA,
             3 * RW_WIDTH + 2 * DECAY_LORA + ICLR_LORA, 3 * RW_WIDTH + 2 * DECAY_LORA + 2 * ICLR_LORA)
MLA_COLS = Q_LORA + KV_LORA + ROPE_DIM
IN_COLS = RW_COLS + MLA_COLS + FT_WIDTH + N_BRANCH * D_MODEL

kernel_name = 'hybrid_rwkv7_mla_fnet_dit'


def rms(x, eps=NORM_EPS):
    xf = x.astype(jnp.float32)
    return (xf * lax.rsqrt(jnp.mean(xf * xf, axis=-1, keepdims=True) + eps)).astype(x.dtype)


def modnorm(x, gain, shift, scale):
    return rms(x) * gain * (1.0 + scale) + shift


def dwconv3(u, w):
    up = jnp.pad(u, ((0, 0), (1, 1), (0, 0)))
    return w[0] * up[:, :-2] + w[1] * up[:, 1:-1] + w[2] * up[:, 2:]


def axial_rope(t, row, col):
    half = ROPE_DIM // 2
    inv = ROPE_BASE ** (-jnp.arange(0, half, 2, dtype=jnp.float32) / half)

    def rot(ta, pos):
        ang = pos.astype(jnp.float32)[:, None] * inv
        cos = jnp.cos(ang)[None, :, None, :]
        sin = jnp.sin(ang)[None, :, None, :]
        x1, x2 = jnp.split(ta.astype(jnp.float32), 2, axis=-1)
        return jnp.concatenate([x1 * cos - x2 * sin, x2 * cos + x1 * sin], axis=-1)

    out = jnp.concatenate([rot(t[..., :half], row), rot(t[..., half:], col)], axis=-1)
    return out.astype(t.dtype)


def rope_tail(t, row, col):
    return jnp.concatenate([t[..., :NOPE_DIM], axial_rope(t[..., NOPE_DIM:], row, col)], axis=-1)


def rwkv_prep(u, p):
    B, L, _ = u.shape
    u = dwconv3(u, p['rw_conv'])
    r, k, v, wd_f, wd_b, ad_f, ad_b, gd = jnp.split(u, RW_SPLITS, axis=-1)
    heads = lambda t: t.reshape(B, L, RW_HEADS, RW_HEAD_DIM)
    kk = heads((k * p['rw_k_k']).astype(jnp.float32))
    kk = kk * lax.rsqrt(jnp.sum(kk * kk, axis=-1, keepdims=True) + 1e-12)
    g = jax.nn.sigmoid(gd) @ p['rw_g_up']
    dirs = []
    for d, (wd, ad) in enumerate(((wd_f, ad_f), (wd_b, ad_b))):
        w_log = -jax.nn.softplus(-(p['rw_w0'][d] + jnp.tanh(wd) @ p['rw_w_up'][d])) - 0.5
        decay = jnp.exp(-jnp.exp(w_log.astype(jnp.float32)))
        a = jax.nn.sigmoid(p['rw_a0'][d] + ad @ p['rw_a_up'][d])
        kd = k * (1.0 + (a - 1.0) * p['rw_k_a'])
        dirs.append((heads(decay), heads(kd), heads(a)))
    return heads(r), heads(v), kk, g, dirs


def _rwkv_step(S, inp):
    r, w, k, v, a, b = inp
    sa = jnp.einsum('bhij,bhj->bhi', S, a)
    S = S * w[:, :, None, :] + sa[..., None] * b[:, :, None, :] + v[..., None] * k[:, :, None, :]
    return S, jnp.einsum('bhij,bhj->bhi', S, r)


def rwkv_scan(S0, seqs, reverse):
    xs = tuple(jnp.moveaxis(s.astype(jnp.float32), 1, 0) for s in seqs)
    S, y = lax.scan(_rwkv_step, S0, xs, reverse=reverse)
    return S, jnp.moveaxis(y, 0, 1)


def rwkv_output(prep, ys, p):
    r, v, kk, g, dirs = prep
    B, L = r.shape[:2]
    y = ys[0] + ys[1]
    mu = jnp.mean(y, axis=-1, keepdims=True)
    var = jnp.mean(jnp.square(y - mu), axis=-1, keepdims=True)
    y = ((y - mu) * lax.rsqrt(var + RW_LN_EPS)).reshape(B, L, RW_WIDTH).astype(r.dtype)
    y = y * p['rw_ln_w'] + p['rw_ln_b']
    bonus = sum(jnp.sum(r * dirs[d][1] * p['rw_r_k'], axis=-1, keepdims=True) * v for d in range(2))
    return (y + bonus.reshape(B, L, RW_WIDTH)) * g


def rwkv_mixer(u_lat, u_ctx, p, need_ctx):
    prep_c = rwkv_prep(u_ctx, p)
    prep_l = rwkv_prep(u_lat, p)
    S0 = jnp.zeros((u_lat.shape[0], RW_HEADS, RW_HEAD_DIM, RW_HEAD_DIM), jnp.float32)
    ys_c, ys_l = [], []
    for d in range(2):
        def scan_inputs(prep):
            r, v, kk, g, dirs = prep
            decay, kd, a = dirs[d]
            return (r, decay, kd, v, -kk, kk * a)
        S_c, y_c = rwkv_scan(S0, scan_inputs(prep_c), d == 1)
        _, y_l = rwkv_scan(S_c, scan_inputs(prep_l), d == 1)
        ys_c.append(y_c)
        ys_l.append(y_l)
    out_l = rwkv_output(prep_l, ys_l, p)
    out_c = rwkv_output(prep_c, ys_c, p) if need_ctx else None
    return out_l, out_c


def mla_q(u, p, row, col):
    B, L, _ = u.shape
    cq = rms(u[..., :Q_LORA]) * p['mla_q_norm']
    q = (cq @ p['w_uq']).reshape(B, L, MLA_HEADS, QK_DIM)
    q = rms(q) * p['q_gain']
    return q if row is None else rope_tail(q, row, col)


def mla_kv(u, p, row, col):
    B, L, _ = u.shape
    ckv = rms(u[..., Q_LORA:Q_LORA + KV_LORA]) * p['mla_kv_norm']
    kv = (ckv @ p['w_ukv']).reshape(B, L, MLA_HEADS, NOPE_DIM + V_DIM)
    k_rope = jnp.broadcast_to(u[..., None, Q_LORA + KV_LORA:], (B, L, MLA_HEADS, ROPE_DIM))
    k = rms(jnp.concatenate([kv[..., :NOPE_DIM], k_rope], axis=-1)) * p['k_gain']
    if row is not None:
        k = rope_tail(k, row, col)
    return k, kv[..., NOPE_DIM:]


def attend(q, k, v):
    s = jnp.einsum('bqhd,bkhd->bhqk', q, k).astype(jnp.float32) * (QK_DIM ** -0.5)
    pr = jax.nn.softmax(s, axis=-1).astype(v.dtype)
    return jnp.einsum('bhqk,bkhd->bqhd', pr, v)


def blocked_attend(q, k, v):
    B, L, H, Dk = q.shape
    nb = L // Q_BLOCK
    qb = jnp.moveaxis(q.reshape(B, nb, Q_BLOCK, H, Dk), 1, 0)
    o = lax.map(lambda qq: attend(qq, k, v), qb)
    return jnp.moveaxis(o, 0, 1).reshape(B, L, H * V_DIM)


def fourier_mix(u):
    B, L, _ = u.shape
    z = jnp.fft.fft2(u.astype(jnp.float32).reshape(B, L, FT_GROUPS, FT_GROUP_DIM), axes=(1, 3), norm='ortho')
    return jnp.real(z).reshape(B, L, FT_WIDTH).astype(u.dtype)


def split_cols(u):
    o1 = RW_COLS
    o2 = o1 + MLA_COLS
    o3 = o2 + FT_WIDTH
    return u[..., :o1], u[..., o1:o2], u[..., o2:o3], u[..., o3:]


def merge(gate_u, yA, yB, yC, p):
    gA, gB, gC = jnp.split(jax.nn.sigmoid(gate_u), N_BRANCH, axis=-1)
    m = gA * (yA @ p['w_branch_a']) + gB * (yB @ p['w_branch_b']) + gC * (yC @ p['w_branch_c'])
    return m @ p['w_out']


def token_mixing(h_lat, h_ctx, p, row, col, last):
    B, L, _ = h_lat.shape
    a_lat, b_lat, f_lat, gate_lat = split_cols(h_lat @ p['w_in'])
    a_ctx, b_ctx, f_ctx, gate_ctx = split_cols(h_ctx @ p['w_in'])
    yA_lat, yA_ctx = rwkv_mixer(a_lat, a_ctx, p, not last)
    k_c, v_c = mla_kv(b_ctx, p, None, None)
    k_l, v_l = mla_kv(b_lat, p, row, col)
    q_l = mla_q(b_lat, p, row, col)
    yB_lat = blocked_attend(q_l, jnp.concatenate([k_c, k_l], axis=1), jnp.concatenate([v_c, v_l], axis=1))
    out_lat = merge(gate_lat, yA_lat, yB_lat, fourier_mix(f_lat), p)
    if last:
        return out_lat, None
    q_c = mla_q(b_ctx, p, None, None)
    yB_ctx = attend(q_c, k_c, v_c).reshape(B, h_ctx.shape[1], MLA_WIDTH)
    out_ctx = merge(gate_ctx, yA_ctx, yB_ctx, fourier_mix(f_ctx), p)
    return out_lat, out_ctx


def conv_ffn(h, p):
    u = dwconv3(h @ p['w_up'], p['ffn_conv'])
    a, b = jnp.split(u, 2, axis=-1)
    return (jax.nn.silu(a) * b) @ p['w_down']


def layer(x_lat, x_ctx, mod_lat, mod_ctx, p, row, col, last):
    sh1_l, sc1_l, gt1_l, sh2_l, sc2_l, gt2_l = jnp.split(mod_lat, N_MOD, axis=-1)
    sh1_c, sc1_c, gt1_c, sh2_c, sc2_c, gt2_c = jnp.split(mod_ctx, N_MOD, axis=-1)
    h_lat = modnorm(x_lat, p['g_norm1'], sh1_l, sc1_l)
    h_ctx = modnorm(x_ctx, p['g_norm1'], sh1_c, sc1_c)
    m_lat, m_ctx = token_mixing(h_lat, h_ctx, p, row, col, last)
    x_lat = x_lat + gt1_l * m_lat
    x_lat = x_lat + gt2_l * conv_ffn(modnorm(x_lat, p['g_norm2'], sh2_l, sc2_l), p)
    if not last:
        x_ctx = x_ctx + gt1_c * m_ctx
        x_ctx = x_ctx + gt2_c * conv_ffn(modnorm(x_ctx, p['g_norm2'], sh2_c, sc2_c), p)
    return x_lat, x_ctx


def setup_inputs(seed: int = 0) -> dict:
    key = jax.random.key(seed)
    ks = iter(jax.random.split(key, 40))
    nrm = lambda shape, s: s * jax.random.normal(next(ks), shape, jnp.float32)
    conv_base = jnp.array([0.25, 0.5, 0.25], jnp.float32)[:, None]
    return {
        'x': nrm((BATCH, SEQ, D_MODEL), 1.0),
        'c': nrm((BATCH, D_MODEL), 1.0),
        'ctx': nrm((BATCH, CTX_LEN, D_MODEL), 1.0),
        'c_ctx': nrm((D_MODEL,), 1.0),
        'w_ada': nrm((DEPTH, D_MODEL, N_MOD * D_MODEL), 0.5 * D_MODEL ** -0.5),
        'b_ada': nrm((DEPTH, N_MOD * D_MODEL), 0.01),
        'g_norm1': 1.0 + nrm((DEPTH, D_MODEL), 0.02),
        'g_norm2': 1.0 + nrm((DEPTH, D_MODEL), 0.02),
        'w_in': nrm((DEPTH, D_MODEL, IN_COLS), D_MODEL ** -0.5),
        'rw_conv': conv_base[None] + nrm((DEPTH, 3, RW_COLS), 0.1),
        'rw_w0': nrm((DEPTH, 2, RW_WIDTH), 0.5),
        'rw_w_up': nrm((DEPTH, 2, DECAY_LORA, RW_WIDTH), 0.3 * DECAY_LORA ** -0.5),
        'rw_a0': nrm((DEPTH, 2, RW_WIDTH), 0.5),
        'rw_a_up': nrm((DEPTH, 2, ICLR_LORA, RW_WIDTH), 0.5 * ICLR_LORA ** -0.5),
        'rw_g_up': nrm((DEPTH, GATE_LORA, RW_WIDTH), GATE_LORA ** -0.5),
        'rw_k_k': 1.0 + nrm((DEPTH, RW_WIDTH), 0.1),
        'rw_k_a': 1.0 + nrm((DEPTH, RW_WIDTH), 0.1),
        'rw_r_k': nrm((DEPTH, RW_HEADS, RW_HEAD_DIM), 0.1),
        'rw_ln_w': 1.0 + nrm((DEPTH, RW_WIDTH), 0.02),
        'rw_ln_b': nrm((DEPTH, RW_WIDTH), 0.02),
        'mla_q_norm': 1.0 + nrm((DEPTH, Q_LORA), 0.02),
        'w_uq': nrm((DEPTH, Q_LORA, MLA_HEADS * QK_DIM), Q_LORA ** -0.5),
        'mla_kv_norm': 1.0 + nrm((DEPTH, KV_LORA), 0.02),
        'w_ukv': nrm((DEPTH, KV_LORA, MLA_HEADS * (NOPE_DIM + V_DIM)), KV_LORA ** -0.5),
        'q_gain': 1.0 + nrm((DEPTH, QK_DIM), 0.02),
        'k_gain': 1.0 + nrm((DEPTH, QK_DIM), 0.02),
        'w_branch_a': nrm((DEPTH, RW_WIDTH, D_MODEL), RW_WIDTH ** -0.5),
        'w_branch_b': nrm((DEPTH, MLA_WIDTH, D_MODEL), MLA_WIDTH ** -0.5),
        'w_branch_c': nrm((DEPTH, FT_WIDTH, D_MODEL), FT_WIDTH ** -0.5),
        'w_out': nrm((DEPTH, D_MODEL, D_MODEL), D_MODEL ** -0.5),
        'ffn_conv': conv_base[None] + nrm((DEPTH, 3, 2 * D_FF), 0.1),
        'w_up': nrm((DEPTH, D_MODEL, 2 * D_FF), D_MODEL ** -0.5),
        'w_down': nrm((DEPTH, D_FF, D_MODEL), D_FF ** -0.5),
    }


def reference(x, c, ctx, c_ctx, w_ada, b_ada, g_norm1, g_norm2, w_in, rw_conv, rw_w0, rw_w_up, rw_a0,
              rw_a_up, rw_g_up, rw_k_k, rw_k_a, rw_r_k, rw_ln_w, rw_ln_b, mla_q_norm, w_uq, mla_kv_norm,
              w_ukv, q_gain, k_gain, w_branch_a, w_branch_b, w_branch_c, w_out, ffn_conv, w_up, w_down):
    L = x.shape[1]
    rows = L // GRID_W
    row = jnp.repeat(jnp.arange(rows, dtype=jnp.int32), GRID_W)
    col = jnp.tile(jnp.arange(GRID_W, dtype=jnp.int32), rows)
    x_lat, x_ctx = x, ctx
    for l in range(DEPTH):
        p = dict(g_norm1=g_norm1[l], g_norm2=g_norm2[l], w_in=w_in[l], rw_conv=rw_conv[l],
                 rw_w0=rw_w0[l], rw_w_up=rw_w_up[l], rw_a0=rw_a0[l], rw_a_up=rw_a_up[l],
                 rw_g_up=rw_g_up[l], rw_k_k=rw_k_k[l], rw_k_a=rw_k_a[l], rw_r_k=rw_r_k[l],
                 rw_ln_w=rw_ln_w[l], rw_ln_b=rw_ln_b[l], mla_q_norm=mla_q_norm[l], w_uq=w_uq[l],
                 mla_kv_norm=mla_kv_norm[l], w_ukv=w_ukv[l], q_gain=q_gain[l], k_gain=k_gain[l],
                 w_branch_a=w_branch_a[l], w_branch_b=w_branch_b[l], w_branch_c=w_branch_c[l],
                 w_out=w_out[l], ffn_conv=ffn_conv[l], w_up=w_up[l], w_down=w_down[l])
        mod_lat = (jax.nn.silu(c) @ w_ada[l] + b_ada[l])[:, None, :]
        mod_ctx = (jax.nn.silu(c_ctx) @ w_ada[l] + b_ada[l])[None, None, :]
        x_lat, x_ctx = layer(x_lat, x_ctx, mod_lat, mod_ctx, p, row, col, l == DEPTH - 1)
    return x_lat
```

```python
import time, os
import numpy as np
import ml_dtypes
import concourse.bass as bass
import concourse.mybir as mybir
from concourse.bass_utils import run_bass_kernel_spmd


F32 = mybir.dt.float32
BF16 = mybir.dt.bfloat16
AF = mybir.ActivationFunctionType
ALU = mybir.AluOpType
AX = mybir.AxisListType

ENGS = ("pe", "act", "dve", "pool", "sp")
NDMASEM = 12


class Buf:
    def __init__(self, name, t, excl=False):
        self.name = name
        self.t = t
        self.writer = None
        self.readers = {}
        self.excl = excl

    def __getitem__(self, idx):
        return self.t[idx]


class Prog:
    def __init__(self, nc, same_engine_sync=True):
        self.nc = nc
        self.q = {e: [] for e in ENGS}
        self.cnt = {e: 0 for e in ENGS}
        self.dma_cnt = {e: 0 for e in ENGS}
        self.seen = {e: {} for e in ENGS}
        self.same = same_engine_sync
        self.stack = []
        self.bufs = []
        self.sem_stack = []
        self.sems = {}
        for e in ENGS:
            self._mksem(("c", e))
        for e in ("sp", "pool", "act"):
            for i in range(NDMASEM):
                self._mksem(("d", e, i))

    def _mksem(self, k):
        cm = self.nc.semaphore("s_" + "_".join(str(x) for x in k))
        self.sems[k] = cm.__enter__()
        self.sem_stack.append(cm)

    def sbuf(self, name, shape, dt=F32):
        cm = self.nc.sbuf_tensor(name, list(shape), dt)
        t = cm.__enter__()
        self.stack.append(cm)
        b = Buf(name, t)
        self.bufs.append(b)
        return b

    def psum(self, name, shape, dt=F32):
        cm = self.nc.psum_tensor(name, list(shape), dt)
        t = cm.__enter__()
        self.stack.append(cm)
        b = Buf(name, t, excl=True)
        self.bufs.append(b)
        return b

    def dram(self, name, shape, dt=F32, kind="Internal"):
        t = self.nc.dram_tensor(name, list(shape), dt, kind=kind)
        b = Buf(name, t.ap())
        self.bufs.append(b)
        return b

    def _deps(self, eng, reads, writes):
        deps = []
        for b in reads:
            if b.writer is not None:
                deps.append(b.writer)
            if getattr(b, "excl", False):
                for k_, tok in b.readers.items():
                    if tok[1] != eng:
                        deps.append(tok)
        for b in writes:
            if b.writer is not None:
                deps.append(b.writer)
            deps.extend(b.readers.values())
        waits = []
        seen = self.seen[eng]
        for (kind, e2, key, val) in deps:
            if kind == "c" and e2 == eng and (eng == "pe" or not self.same):
                continue
            if seen.get(key, 0) >= val:
                continue
            seen[key] = val
            waits.append((key, val))
        return waits

    def op(self, eng, fn, reads=(), writes=()):
        waits = self._deps(eng, reads, writes)
        self.cnt[eng] += 1
        tok = ("c", eng, ("c", eng), self.cnt[eng])
        self.q[eng].append((waits, fn, (("c", eng), 1)))
        for b in reads:
            b.readers[eng] = tok
        for b in writes:
            b.writer = tok
            b.readers = {}
        return tok

    def I(self, eng, fname, reads=(), writes=(), **kw):
        return self.op(eng, lambda e, fname=fname, kw=kw: getattr(e, fname)(**kw), reads=reads, writes=writes)

    def dma(self, eng, out_ap, in_ap, reads=(), writes=(), **kw):
        j = self.dma_cnt[eng]
        self.dma_cnt[eng] += 1
        key = ("d", eng, j % NDMASEM)
        n_prev = j // NDMASEM
        waits = self._deps(eng, reads, writes)
        seen = self.seen[eng]
        if n_prev > 0 and seen.get(key, 0) < 16 * n_prev:
            seen[key] = 16 * n_prev
            waits.append((key, 16 * n_prev))
        tok = ("d", eng, key, 16 * (n_prev + 1))

        def fn(e, out_ap=out_ap, in_ap=in_ap, kw=kw):
            return e.dma_start(out=out_ap, in_=in_ap, **kw)
        self.q[eng].append((waits, fn, (key, 16)))
        for b in reads:
            b.readers[("dma", eng, j % NDMASEM)] = tok
        for b in writes:
            b.writer = tok
            b.readers = {}
        return tok

    def flush(self, fin=()):
        nc = self.nc
        sems = self.sems
        q = self.q
        engobj = {"pe": "tensor", "act": "scalar", "dve": "vector", "pool": "gpsimd", "sp": "sync"}
        if not any(q[e] for e in ENGS) and not fin:
            return
        with nc.Block() as block:
            def make(e):
                def body(eng):
                    for (waits, fn, inc) in q[e]:
                        for (k, v) in waits:
                            eng.wait_ge(sems[k], v)
                        ins = fn(eng)
                        ins.then_inc(sems[inc[0]], inc[1])
                    if e == "sp":
                        for (k, v) in fin:
                            eng.wait_ge(sems[k], v)
                return body
            for e in ENGS:
                if q[e] or (e == "sp" and fin):
                    getattr(block, engobj[e])(make(e))
        self.q = {e: [] for e in ENGS}

    def emit(self, final_bufs=()):
        fin = []
        for b in final_bufs:
            if b.writer is not None:
                fin.append((b.writer[2], b.writer[3]))
        self.flush(fin)
        for cm in reversed(self.stack):
            cm.__exit__(None, None, None)
        self.stack = []
        for cm in reversed(self.sem_stack):
            cm.__exit__(None, None, None)
        self.sem_stack = []

    def stats(self):
        return {e: (self.cnt[e], self.dma_cnt[e]) for e in ENGS}


def _barrier(self):
    outstanding = {}
    for b in self.bufs:
        toks = list(b.readers.values())
        if b.writer is not None:
            toks.append(b.writer)
        for (kind, e2, key, val) in toks:
            if outstanding.get(key, 0) < val:
                outstanding[key] = val
        b.writer = None
        b.readers = {}
    for e in ENGS:
        if self.cnt[e] > 0:
            key = ("c", e)
            if outstanding.get(key, 0) < self.cnt[e]:
                outstanding[key] = self.cnt[e]
    seen = self.seen["sp"]
    waits = []
    for key, val in outstanding.items():
        if key == ("c", "sp"):
            continue
        if seen.get(key, 0) < val:
            seen[key] = val
            waits.append((key, val))
    self.cnt["sp"] += 1
    fence_val = self.cnt["sp"]
    self.q["sp"].append((waits, lambda e: e.nop(), (("c", "sp"), 1)))
    for e in ENGS:
        if e == "sp":
            continue
        self.seen[e][("c", "sp")] = fence_val
        self.cnt[e] += 1
        self.q[e].append(([(("c", "sp"), fence_val)], lambda en: en.nop(), (("c", e), 1)))


Prog.barrier = _barrier


class _Scope:
    def __init__(self, P):
        self.P = P

    def __enter__(self):
        self.mark = len(self.P.stack)
        self.bmark = len(self.P.bufs)
        return self

    def __exit__(self, *a):
        P = self.P
        P.barrier()
        P.flush()
        while len(P.stack) > self.mark:
            cm = P.stack.pop()
            cm.__exit__(None, None, None)
        del P.bufs[self.bmark:]
        return False


def _scope(self):
    if not hasattr(self, "deferred_exit"):
        self.deferred_exit = []
    return _Scope(self)


Prog.scope = _scope


D = 1024
DFF = 2816
TPC = 2048
NT = 9
TW = 256
EPS = 1e-6


def load_cast(P, dst, dst_ap_fn, src_ap, nparts, piece, engs=("pool",), stage_bufs=None, dma_eng="sp"):
    n = nparts
    i = 0
    c0 = 0
    while c0 < n:
        c1 = min(n, c0 + piece)
        sb = stage_bufs[i % len(stage_bufs)]
        P.dma(dma_eng, sb[:, 0:c1 - c0], src_ap[:, c0:c1], writes=[sb])
        eng = engs[i % len(engs)]
        if eng == "act":
            P.I("act", "activation", reads=[sb], writes=[dst], out=dst_ap_fn(c0, c1), in_=sb[:, 0:c1 - c0], func=AF.Copy)
        else:
            P.I(eng, "tensor_copy", reads=[sb], writes=[dst], out=dst_ap_fn(c0, c1), in_=sb[:, 0:c1 - c0])
        c0 = c1
        i += 1


def modnorm_tile(P, xt, h, ncols, gsc, sh, j, ones, ps_n, sq, rstd, hn, fl=None, ti=None, eps_t=None, nk=8, dmodel=D):
    P.I("act", "activation", reads=[xt], writes=[sq], out=sq[:, :, 0:ncols], in_=xt[:, :, 0:ncols], func=AF.Square)
    for k in range(nk):
        P.I("pe", "matmul", reads=[ones, sq], writes=[ps_n], out=ps_n[:, 0:ncols], lhsT=ones[:, :], rhs=sq[:, k, 0:ncols],
            start=(k == 0), stop=(k == nk - 1))
    P.I("act", "activation", reads=[ps_n, eps_t], writes=[rstd], out=rstd[:, 0:ncols], in_=ps_n[:, 0:ncols], func=AF.Ln,
        scale=1.0 / dmodel, bias=eps_t[:, 0:1])
    P.I("act", "activation", reads=[rstd], writes=[rstd], out=rstd[:, 0:ncols], in_=rstd[:, 0:ncols], func=AF.Exp, scale=-0.5)
    P.I("dve", "tensor_tensor", reads=[xt, rstd], writes=[hn], out=hn[:, :, 0:ncols], in0=xt[:, :, 0:ncols],
        in1=rstd[:, 0:ncols].unsqueeze(1).to_broadcast([128, nk, ncols]), op=ALU.mult)
    for k in range(nk):
        eng = "pool" if k % 2 == 0 else "dve"
        P.I(eng, "tensor_scalar", reads=[hn, gsc, sh], writes=[h], out=h[:, k, 0:ncols], in0=hn[:, k, 0:ncols],
            scalar1=gsc[:, k, j:j + 1], scalar2=sh[:, k, j:j + 1], op0=ALU.mult, op1=ALU.add)
    if fl is not None:
        P.I("dve", "tensor_scalar", reads=[h, fl], writes=[h], out=h[:, :, 0:1], in0=h[:, :, 0:1], scalar1=fl[:, ti, 0:1],
            scalar2=None, op0=ALU.mult)
        P.I("dve", "tensor_scalar", reads=[h, fl], writes=[h], out=h[:, :, ncols - 1:ncols], in0=h[:, :, ncols - 1:ncols],
            scalar1=fl[:, ti, 1:2], scalar2=None, op0=ALU.mult)


def build_lc():
    nc = bass.Bass("TRN2", target_bir_lowering=False)
    P = Prog(nc)
    di = lambda name, shape, dt=F32: P.dram(name, shape, dt, kind="ExternalInput")
    xT = di("xT", [D, TPC + 2])
    cT = di("cT", [D, TW + 2])
    mod = di("mod", [128, 48, 2])
    g2 = di("g2", [128, 8])
    flags = di("flags", [128, NT, 2])
    wup = di("wup", [128, 8 * 2 * DFF])
    wdn = di("wdn", [128, 22 * D])
    fconv = di("fconv", [128, 44, 3])
    xo = P.dram("xo", [D, TPC], F32, kind="ExternalOutput")
    co = P.dram("co", [D, TW], F32, kind="ExternalOutput")

    ps = [P.psum(f"ps{i}", [128, 512], F32) for i in range(8)]
    wup_b = P.sbuf("wup_b", [128, 8, 2 * DFF], BF16)
    wdn_d = P.dram("wdn_bf", [128, 22, D], BF16)
    wdn_p = [P.sbuf(f"wdn_p{i}", [128, 22, 128], BF16) for i in range(2)]
    stage = [P.sbuf(f"stage{i}", [128, 1408], F32) for i in range(2)]
    stage_b = [P.sbuf(f"stageb{i}", [128, 1408], BF16) for i in range(2)]
    modt = P.sbuf("modt", [128, 48, 2])
    g2t = P.sbuf("g2t", [128, 8])
    flt = P.sbuf("flt", [128, NT, 2])
    fct = P.sbuf("fct", [128, 44, 3])
    gsc = P.sbuf("gsc", [128, 8, 2])
    ones = P.sbuf("ones", [128, 128])
    eps_t = P.sbuf("eps_t", [128, 1])
    P.dma("sp", modt[:], mod[:], writes=[modt])
    P.dma("sp", g2t[:], g2[:], writes=[g2t])
    P.dma("sp", flt[:], flags[:], writes=[flt])
    P.dma("sp", fct[:], fconv[:], writes=[fct])
    P.I("dve", "memset", writes=[ones], ap=ones[:], constant=1.0)
    P.I("dve", "memset", writes=[eps_t], ap=eps_t[:], constant=EPS)
    P.I("dve", "tensor_scalar", reads=[modt], writes=[gsc], out=gsc[:], in0=modt[:, 32:40, :], scalar1=1.0, scalar2=None, op0=ALU.add)
    P.I("dve", "tensor_tensor", reads=[gsc, g2t], writes=[gsc], out=gsc[:], in0=gsc[:],
        in1=g2t[:].unsqueeze(2).to_broadcast([128, 8, 2]), op=ALU.mult)
    wup_flat = wup_b[:].rearrange("p k m -> p (k m)")
    wdn_flat = wdn_d[:].rearrange("p k m -> p (k m)")
    load_cast(P, wup_b, lambda a, b: wup_flat[:, a:b], wup, 8 * 2 * DFF, 1408, engs=("pool", "act"), stage_bufs=stage)
    for i in range(16):
        c0, c1 = i * 1408, (i + 1) * 1408
        sbf, sbb = stage[i % 2], stage_b[i % 2]
        P.dma("sp", sbf[:], wdn[:, c0:c1], writes=[sbf])
        P.I("pool", "tensor_copy", reads=[sbf], writes=[sbb], out=sbb[:], in_=sbf[:])
        P.dma("sp", wdn_flat[:, c0:c1], sbb[:], reads=[sbb], writes=[wdn_d])

    xb = P.sbuf("xt0", [128, 8, TW + 2])
    sq = P.sbuf("sq", [128, 8, TW + 2])
    rstd = P.sbuf("rstd", [128, TW + 2])
    hn = sq
    h = [P.sbuf(f"h{i}", [128, 8, TW + 2], BF16) for i in range(2)]
    ca = [P.sbuf(f"ca{i}", [128, TW]) for i in range(2)]
    cb = [P.sbuf(f"cb{i}", [128, TW]) for i in range(2)]
    sa = [P.sbuf(f"sa{i}", [128, TW]) for i in range(2)]
    sb = P.sbuf("st0", [128, 22, TW], BF16)
    ob = P.sbuf("xot0", [128, 8, TW])
    NC = TW + 2
    sh2 = _Sub(modt, 24)
    for ti in range(NT):
        j = 0 if ti < 8 else 1
        hb = h[ti % 2]
        if ti < 8:
            src = xT[:].rearrange("(k p) t -> p k t", p=128)[:, :, ti * TW: ti * TW + NC]
        else:
            src = cT[:].rearrange("(k p) t -> p k t", p=128)
        P.dma("sp", xb[:], src, reads=[xT, cT], writes=[xb])
        modnorm_tile(P, xb, hb, NC, gsc, sh2, j, ones, ps[7], sq, rstd, hn, fl=flt, ti=ti, eps_t=eps_t)
        for jj in range(22):
            pa, pb = ps[(2 * jj) % 4], ps[(2 * jj + 1) % 4]
            for (m, pp) in ((jj, pa), (22 + jj, pb)):
                for k in range(8):
                    P.I("pe", "matmul", reads=[wup_b, hb], writes=[pp], out=pp[:, 0:NC], lhsT=wup_b[:, k, m * 128:(m + 1) * 128],
                        rhs=hb[:, k, :], start=(k == 0), stop=(k == 7))
            a_, b_, s_ = ca[jj % 2], cb[jj % 2], sa[jj % 2]
            for (m, pp, cc) in ((jj, pa, a_), (22 + jj, pb, b_)):
                P.I("act", "activation", reads=[pp, fct], writes=[cc], out=cc[:], in_=pp[:, 1:TW + 1], func=AF.Copy, scale=fct[:, m, 1:2])
                P.I("dve", "scalar_tensor_tensor", reads=[pp, fct, cc], writes=[cc], out=cc[:], in0=pp[:, 0:TW], scalar=fct[:, m, 0:1],
                    in1=cc[:], op0=ALU.mult, op1=ALU.add)
                P.I("dve", "scalar_tensor_tensor", reads=[pp, fct, cc], writes=[cc], out=cc[:], in0=pp[:, 2:TW + 2], scalar=fct[:, m, 2:3],
                    in1=cc[:], op0=ALU.mult, op1=ALU.add)
            P.I("act", "activation", reads=[a_], writes=[s_], out=s_[:], in_=a_[:], func=AF.Silu)
            P.I("pool", "tensor_tensor", reads=[s_, b_], writes=[sb], out=sb[:, jj, :], in0=s_[:], in1=b_[:], op=ALU.mult)
        for m in range(8):
            pp = ps[4 + m % 2]
            wp = wdn_p[m % 2]
            P.dma("pool", wp[:], wdn_d[:, :, m * 128:(m + 1) * 128], reads=[wdn_d], writes=[wp])
            for k in range(22):
                P.I("pe", "matmul", reads=[wp, sb], writes=[pp], out=pp[:, 0:TW], lhsT=wp[:, k, :], rhs=sb[:, k, :],
                    start=(k == 0), stop=(k == 21))
            P.I("dve", "scalar_tensor_tensor", reads=[pp, modt, xb], writes=[ob], out=ob[:, m, :], in0=pp[:, 0:TW],
                scalar=modt[:, 40 + m, j:j + 1], in1=xb[:, m, 1:TW + 1], op0=ALU.mult, op1=ALU.add)
        if ti < 8:
            dst = xo[:].rearrange("(k p) t -> p k t", p=128)[:, :, ti * TW:(ti + 1) * TW]
        else:
            dst = co[:].rearrange("(k p) t -> p k t", p=128)
        P.dma("sp", dst, ob[:], reads=[ob], writes=[xo, co])
    P.emit(final_bufs=[xo, co])
    print("LC stats", P.stats())
    return nc


class _Sub:
    def __init__(self, base, off):
        self.base = base
        self.off = off
        self.name = base.name

    def __getitem__(self, idx):
        idx = list(idx)
        k = idx[1]
        if isinstance(k, slice):
            idx[1] = slice(k.start + self.off, k.stop + self.off)
        else:
            idx[1] = k + self.off
        return self.base.t[tuple(idx)]

    @property
    def writer(self):
        return self.base.writer

    @writer.setter
    def writer(self, v):
        self.base.writer = v

    @property
    def readers(self):
        return self.base.readers

    @readers.setter
    def readers(self, v):
        self.base.readers = v


def lay128(v, nchunk):
    return np.ascontiguousarray(v.reshape(nchunk, 128).T)


def run_lc(nc, xm_lat, xm_ctx, mod_l, g2, w_up, ffn_conv, w_down):
    wup_l = np.ascontiguousarray(w_up.reshape(8, 128, 2 * DFF).transpose(1, 0, 2).reshape(128, -1))
    wdn_l = np.ascontiguousarray(w_down.reshape(22, 128, D).transpose(1, 0, 2).reshape(128, -1))
    fconv_l = np.ascontiguousarray(ffn_conv.reshape(3, 44, 128).transpose(2, 1, 0))
    g2_l = lay128(g2, 8)
    xpad = np.zeros((D, 16384 + 2), np.float32)
    xpad[:, 1:-1] = xm_lat.T
    cpad = np.zeros((D, TW + 2), np.float32)
    cpad[:, 1:-1] = xm_ctx.T
    in_maps = []
    for i in range(8):
        fl = np.ones((128, NT, 2), np.float32)
        fl[:, 8, :] = 0
        if i == 0:
            fl[:, 0, 0] = 0
        if i == 7:
            fl[:, 7, 1] = 0
        in_maps.append({"xT": np.ascontiguousarray(xpad[:, i * TPC: i * TPC + TPC + 2]), "cT": cpad, "mod": mod_l, "g2": g2_l,
                        "flags": fl, "wup": wup_l, "wdn": wdn_l, "fconv": fconv_l})
    res = run_bass_kernel_spmd(nc, in_maps, core_ids=list(range(8)))
    xo = np.concatenate([r["xo"] for r in res.results], axis=1).T
    co = res.results[0]["co"].T
    return np.ascontiguousarray(xo), np.ascontiguousarray(co)


D = 1024
TPC = 2048
TW = 256
NT = 9
NTOK = NT * TW
EPS = 1e-6
NCH = 44
def chunk_cols():
    ch = []
    for i in range(9):
        ch.append(np.arange(i * 128, (i + 1) * 128))
    ch.append(np.concatenate([np.arange(1152, 1216), np.arange(1280, 1344)]))
    ch.append(np.concatenate([np.arange(1216, 1280), np.arange(1344, 1408)]))
    ch.append(np.arange(1408, 1536))
    for i in range(3):
        ch.append(np.arange(1536 + i * 128, 1536 + (i + 1) * 128))
    for i in range(2):
        ch.append(np.arange(1920 + i * 128, 1920 + (i + 1) * 128))
    ch.append(np.concatenate([np.arange(2112, 2208), np.arange(2112, 2144)]))
    for i in range(2):
        ch.append(np.arange(2208 + i * 128, 2208 + (i + 1) * 128))
    for i in range(24):
        ch.append(np.arange(2464 + i * 128, 2464 + (i + 1) * 128))
    return ch


CH_A = list(range(12)) + [15, 16, 17, 18, 19]
CH_B = list(range(15)) + list(range(20, 44))


def phase_proj(P, ps, xT, cT, modt, g1t, flt, win_d, chs, U, rwc, ones, eps_t):
    nch = len(chs)
    with P.scope():
        wb = P.sbuf("win_b", [128, 8, nch * 128], BF16)
        stage = [P.sbuf(f"pstage{i}", [128, 1024], F32) for i in range(2)]
        wflat = wb[:].rearrange("p k m -> p (k m)")
        i = 0
        for k in range(8):
            for ci, c in enumerate(chs):
                sb = stage[i % 2]
                i += 0
        runs = []
        s0 = 0
        while s0 < nch:
            s1 = s0
            while s1 + 1 < nch and chs[s1 + 1] == chs[s1] + 1 and (s1 + 1 - s0) < 8:
                s1 += 1
            runs.append((s0, s1 + 1))
            s0 = s1 + 1
        i = 0
        for k in range(8):
            for (a, b) in runs:
                n = (b - a) * 128
                sb = stage[i % 2]
                P.dma("sp", sb[:, 0:n], win_d[:, k, chs[a] * 128: chs[a] * 128 + n], reads=[win_d], writes=[sb])
                eng = "pool" if i % 2 == 0 else "act"
                if eng == "act":
                    P.I("act", "activation", reads=[sb], writes=[wb], out=wb[:, k, a * 128: a * 128 + n], in_=sb[:, 0:n], func=AF.Copy)
                else:
                    P.I("pool", "tensor_copy", reads=[sb], writes=[wb], out=wb[:, k, a * 128: a * 128 + n], in_=sb[:, 0:n])
                i += 1
        gsc = P.sbuf("gsc1", [128, 8, 2])
        P.I("dve", "tensor_scalar", reads=[modt], writes=[gsc], out=gsc[:], in0=modt[:, 8:16, :], scalar1=1.0, scalar2=None, op0=ALU.add)
        P.I("dve", "tensor_tensor", reads=[gsc, g1t], writes=[gsc], out=gsc[:], in0=gsc[:],
            in1=g1t[:].unsqueeze(2).to_broadcast([128, 8, 2]), op=ALU.mult)
        sh1 = _Sub(modt, 0)
        xb = P.sbuf("p_xt", [128, 8, TW + 2])
        sq = P.sbuf("p_sq", [128, 8, TW + 2])
        rstd = P.sbuf("p_rstd", [128, TW + 2])
        h = [P.sbuf(f"p_h{i}", [128, 8, TW + 2], BF16) for i in range(2)]
        ob = [P.sbuf(f"p_o{i}", [128, TW]) for i in range(4)]
        NC = TW + 2
        oi = 0
        for ti in range(NT):
            j = 0 if ti < 8 else 1
            hb = h[ti % 2]
            if ti < 8:
                src = xT[:].rearrange("(k p) t -> p k t", p=128)[:, :, ti * TW: ti * TW + NC]
            else:
                src = cT[:].rearrange("(k p) t -> p k t", p=128)
            P.dma("sp", xb[:], src, reads=[xT, cT], writes=[xb])
            modnorm_tile(P, xb, hb, NC, gsc, sh1, j, ones, ps[7], sq, rstd, sq, fl=flt, ti=ti, eps_t=eps_t)
            for ci, c in enumerate(chs):
                pp = ps[ci % 4]
                for k in range(8):
                    P.I("pe", "matmul", reads=[wb, hb], writes=[pp], out=pp[:, 0:NC], lhsT=wb[:, k, ci * 128:(ci + 1) * 128],
                        rhs=hb[:, k, :], start=(k == 0), stop=(k == 7))
                o = ob[oi % 4]
                oi += 1
                if c < 12:
                    P.I("act", "activation", reads=[pp, rwc], writes=[o], out=o[:], in_=pp[:, 1:TW + 1], func=AF.Copy, scale=rwc[:, c, 1:2])
                    P.I("dve", "scalar_tensor_tensor", reads=[pp, rwc, o], writes=[o], out=o[:], in0=pp[:, 0:TW], scalar=rwc[:, c, 0:1],
                        in1=o[:], op0=ALU.mult, op1=ALU.add)
                    P.I("dve", "scalar_tensor_tensor", reads=[pp, rwc, o], writes=[o], out=o[:], in0=pp[:, 2:TW + 2], scalar=rwc[:, c, 2:3],
                        in1=o[:], op0=ALU.mult, op1=ALU.add)
                else:
                    if oi % 2 == 0:
                        P.I("act", "activation", reads=[pp], writes=[o], out=o[:], in_=pp[:, 1:TW + 1], func=AF.Copy)
                    else:
                        P.I("dve", "tensor_copy", reads=[pp], writes=[o], out=o[:], in_=pp[:, 1:TW + 1])
                P.dma("pool", U[c, :, ti * TW:(ti + 1) * TW], o[:], reads=[o], writes=[U])


def rms_bcast(P, ps_n, ones_ap, sq, src_ap, rstd, n_k, width, dim, eps_t, ones_buf):
    for k in range(n_k):
        P.I("act", "activation", reads=[src_ap[1]], writes=[sq], out=sq[:, k, 0:width], in_=src_ap[0](k), func=AF.Square)
    rows = ones_ap.shape[0]
    for k in range(n_k):
        P.I("pe", "matmul", reads=[ones_buf, sq], writes=[ps_n], out=ps_n[0:rows, 0:width], lhsT=ones_ap, rhs=sq[0:rows, k, 0:width],
            start=(k == 0), stop=(k == n_k - 1))
    P.I("act", "activation", reads=[ps_n, eps_t], writes=[rstd], out=rstd[0:rows, 0:width], in_=ps_n[0:rows, 0:width], func=AF.Ln,
        scale=1.0 / dim, bias=eps_t[0:rows, 0:1])
    P.I("act", "activation", reads=[rstd], writes=[rstd], out=rstd[0:rows, 0:width], in_=rstd[0:rows, 0:width], func=AF.Exp, scale=-0.5)


def phase_kv(P, ps, U, kvn_t, wukv_d, kgain_t, rott, cos_d, sin_d, ones, eps_t, kT_out, v1_out):
    with P.scope():
        wk = P.sbuf("wukv_sb", [128, 2, 768])
        P.dma("sp", wk[:], wukv_d[:], reads=[wukv_d], writes=[wk])
        u = P.sbuf("kv_u", [128, 3, TW])
        sq = P.sbuf("kv_sq", [128, 2, TW])
        rstd = P.sbuf("kv_rstd", [128, TW])
        ckv = P.sbuf("kv_ckv", [128, 2, TW])
        kcat = [P.sbuf(f"kv_kcat{i}", [96, TW]) for i in range(2)]
        ksq = P.sbuf("kv_ksq", [96, 1, TW])
        krstd = P.sbuf("kv_krstd", [96, TW])
        kn = P.sbuf("kv_kn", [96, TW])
        t1 = P.sbuf("kv_t1", [96, TW])
        cs = P.sbuf("kv_cs", [96, 2, TW])
        kb = [P.sbuf(f"kv_kb{i}", [96, TW], BF16) for i in range(2)]
        vb = [P.sbuf(f"kv_vb{i}", [128, 6, 65], BF16) for i in range(2)]
        for i in range(2):
            P.I("dve", "memset", writes=[vb[i]], ap=vb[i][:], constant=1.0)
        for ti in range(NT):
            for c in range(3):
                P.dma("sp", u[:, c, :], U[15 + c, :, ti * TW:(ti + 1) * TW], reads=[U], writes=[u])
            rms_bcast(P, ps[7], ones[:, :], sq, (lambda k: u[:, k, :], u), rstd, 2, TW, 256.0, eps_t, ones)
            for k in range(2):
                P.I("dve", "scalar_tensor_tensor", reads=[u, kvn_t, rstd], writes=[ckv], out=ckv[:, k, :], in0=u[:, k, :],
                    scalar=kvn_t[:, k:k + 1], in1=rstd[:, :], op0=ALU.mult, op1=ALU.mult)
            if ti < 8:
                P.dma("sp", cs[:, 0, :], cos_d[:, ti * TW:(ti + 1) * TW], reads=[cos_d], writes=[cs])
                P.dma("sp", cs[:, 1, :], sin_d[:, ti * TW:(ti + 1) * TW], reads=[sin_d], writes=[cs])
            for hf in range(2):
                pv = ps[4 + hf]
                for k in range(2):
                    P.I("pe", "matmul", reads=[ckv, wk], writes=[pv], out=pv[:, 0:384], lhsT=ckv[:, k, hf * 128:(hf + 1) * 128],
                        rhs=wk[:, k, :].rearrange("p (h c) -> p h c", c=128)[:, :, 64:128], start=(k == 0), stop=(k == 1))
                v_ = vb[hf]
                P.I("dve", "tensor_copy", reads=[pv], writes=[v_], out=v_[:, :, 0:64], in_=pv[:, 0:384].rearrange("p (h c) -> p h c", c=64))
                P.dma("pool", v1_out[ti * TW + hf * 128: ti * TW + (hf + 1) * 128, :, :], v_[:], reads=[v_], writes=[v1_out])
            for hh in range(6):
                pk = ps[hh % 2]
                for k in range(2):
                    P.I("pe", "matmul", reads=[ckv, wk], writes=[pk], out=pk[0:64, 0:TW], lhsT=wk[:, k, hh * 128: hh * 128 + 64],
                        rhs=ckv[:, k, :], start=(k == 0), stop=(k == 1))
                kc_ = kcat[hh % 2]
                P.I("act", "activation", reads=[pk], writes=[kc_], out=kc_[0:64, :], in_=pk[0:64, 0:TW], func=AF.Copy)
                P.I("pool", "tensor_copy", reads=[u], writes=[kc_], out=kc_[64:96, :], in_=u[64:96, 2, :])
                rms_bcast(P, ps[6], ones[0:96, 0:96], ksq, (lambda k, kc_=kc_: kc_[:, :], kc_), krstd, 1, TW, 96.0, eps_t, ones)
                P.I("dve", "scalar_tensor_tensor", reads=[kc_, kgain_t, krstd], writes=[kn], out=kn[:, :], in0=kc_[:, :],
                    scalar=kgain_t[0:96, 0:1], in1=krstd[0:96, :], op0=ALU.mult, op1=ALU.mult)
                kb_ = kb[hh % 2]
                if ti < 8:
                    pr = ps[2 + hh % 2]
                    P.I("pe", "matmul", reads=[rott, kn], writes=[pr], out=pr[0:96, 0:TW], lhsT=rott[0:96, 0:96], rhs=kn[:, :],
                        start=True, stop=True)
                    P.I("dve", "tensor_tensor", reads=[pr, cs], writes=[t1], out=t1[:, :], in0=pr[0:96, 0:TW], in1=cs[0:96, 1, :], op=ALU.mult)
                    P.I("pool", "tensor_tensor", reads=[kn, cs], writes=[kn], out=kn[:, :], in0=kn[:, :], in1=cs[0:96, 0, :], op=ALU.mult)
                    P.I("dve", "tensor_tensor", reads=[kn, t1], writes=[kb_], out=kb_[:, :], in0=kn[:, :], in1=t1[:, :], op=ALU.add)
                else:
                    P.I("dve", "tensor_copy", reads=[kn], writes=[kb_], out=kb_[:, :], in_=kn[:, :])
                P.dma("pool", kT_out[hh, :, ti * TW:(ti + 1) * TW], kb_[:, :], reads=[kb_], writes=[kT_out])


def common_inputs(P):
    di = lambda name, shape, dt=F32: P.dram(name, shape, dt, kind="ExternalInput")
    d = {}
    d["xT"] = di("xT", [D, TPC + 2])
    d["cT"] = di("cT", [D, TW + 2])
    d["mod"] = di("mod", [128, 48, 2])
    d["g1"] = di("g1", [128, 8])
    d["flags"] = di("flags", [128, NT, 2])
    d["win"] = di("win", [128, 8, NCH * 128])
    d["rwc"] = di("rwc", [128, 12, 3])
    return d


def build_la(debug=False):
    nc = bass.Bass("TRN2", target_bir_lowering=False)
    P = Prog(nc)
    di = lambda name, shape, dt=F32: P.dram(name, shape, dt, kind="ExternalInput")
    d = common_inputs(P)
    wukv_d = di("wukv", [128, 2, 768])
    kvn_d = di("kvn", [128, 2])
    kgain_d = di("kgain", [128, 1])
    rott_d = di("rott", [128, 128])
    cos_d = di("cosT", [96, TPC])
    sin_d = di("sinT", [96, TPC])
    kT_out = P.dram("kT", [6, 96, NTOK], BF16, kind="ExternalOutput")
    v1_out = P.dram("v1", [NTOK, 6, 65], BF16, kind="ExternalOutput")
    fT_out = P.dram("fT", [2, 128, NTOK], F32, kind="ExternalOutput")
    seg_out = P.dram("seg", [3, 2, 128, 256], F32, kind="ExternalOutput")
    cons = scan_const_inputs(P)
    U = P.dram("U", [NCH, 128, NTOK], F32, kind="ExternalOutput" if debug else "Internal")
    ps = [P.psum(f"ps{i}", [128, 512], F32) for i in range(8)]
    modt = P.sbuf("modt", [128, 48, 2])
    g1t = P.sbuf("g1t", [128, 8])
    flt = P.sbuf("flt", [128, NT, 2])
    rwc = P.sbuf("rwc_t", [128, 12, 3])
    kvn_t = P.sbuf("kvn_t", [128, 2])
    kgain_t = P.sbuf("kgain_t", [128, 1])
    rott = P.sbuf("rott_t", [128, 128])
    ones = P.sbuf("ones", [128, 128])
    eps_t = P.sbuf("eps_t", [128, 1])
    for (t, s) in ((modt, d["mod"]), (g1t, d["g1"]), (flt, d["flags"]), (rwc, d["rwc"]), (kvn_t, kvn_d), (kgain_t, kgain_d), (rott, rott_d)):
        P.dma("sp", t[:], s[:], writes=[t])
    P.I("dve", "memset", writes=[ones], ap=ones[:], constant=1.0)
    P.I("dve", "memset", writes=[eps_t], ap=eps_t[:], constant=EPS)
    phase_proj(P, ps, d["xT"], d["cT"], modt, g1t, flt, d["win"], CH_A, U, rwc, ones, eps_t)
    phase_kv(P, ps, U, kvn_t, wukv_d, kgain_t, rott, cos_d, sin_d, ones, eps_t, kT_out, v1_out)
    with P.scope():
        fb = P.sbuf("fb", [128, 2, NTOK])
        for c in range(2):
            P.dma("sp", fb[:, c, :], U[18 + c, :, :], reads=[U], writes=[fb])
            P.dma("sp", fT_out[c, :, :], fb[:, c, :], reads=[fb], writes=[fT_out])
    scan_pass1(P, ps, U, cons, seg_out)
    outs = [kT_out, v1_out, fT_out, seg_out] + ([U] if debug else [])
    P.emit(final_bufs=outs)
    print("LA stats", P.stats())
    return nc


def rope_tables():
    half = 16
    inv = 10000.0 ** (-np.arange(0, half, 2, dtype=np.float32) / half)
    t = np.arange(16384)
    row = (t // 64).astype(np.float32)
    col = (t % 64).astype(np.float32)
    cosT = np.ones((96, 16384), np.float32)
    sinT = np.zeros((96, 16384), np.float32)
    ar = row[None, :] * inv[:, None]
    ac = col[None, :] * inv[:, None]
    cosT[64:72] = np.cos(ar); cosT[72:80] = np.cos(ar); cosT[80:88] = np.cos(ac); cosT[88:96] = np.cos(ac)
    sinT[64:72] = np.sin(ar); sinT[72:80] = np.sin(ar); sinT[80:88] = np.sin(ac); sinT[88:96] = np.sin(ac)
    Pi = np.zeros((128, 128), np.float32)
    for base in (64, 80):
        for i in range(8):
            Pi[base + i, base + 8 + i] = -1.0
            Pi[base + 8 + i, base + i] = 1.0
    rott = np.ascontiguousarray(Pi.T)
    return cosT, sinT, rott


def host_common(x_lat, x_ctx, mod_l, g1, w_in, rw_conv):
    ch = chunk_cols()
    cols = np.concatenate(ch)
    win_l = np.ascontiguousarray(w_in[:, cols].reshape(8, 128, NCH * 128).transpose(1, 0, 2))
    rwcols = cols[:12 * 128]
    rwc = np.ascontiguousarray(rw_conv[:, rwcols].reshape(3, 12, 128).transpose(2, 1, 0))
    xpad = np.zeros((D, 16384 + 2), np.float32)
    xpad[:, 1:-1] = x_lat.T
    cpad = np.zeros((D, TW + 2), np.float32)
    cpad[:, 1:-1] = x_ctx.T
    g1_l = lay128(g1, 8)
    maps = []
    for i in range(8):
        fl = np.ones((128, NT, 2), np.float32)
        fl[:, 8, :] = 0
        if i == 0:
            fl[:, 0, 0] = 0
        if i == 7:
            fl[:, 7, 1] = 0
        maps.append({"xT": np.ascontiguousarray(xpad[:, i * TPC: i * TPC + TPC + 2]), "cT": cpad, "mod": mod_l, "g1": g1_l,
                     "flags": fl, "win": win_l, "rwc": rwc})
    return maps


def run_la(nc, maps, inp, l, debug=False):
    cosT, sinT, rott = rope_tables()
    wukv = np.ascontiguousarray(inp["w_ukv"][l].reshape(2, 128, 768).transpose(1, 0, 2))
    kvn = lay128(inp["mla_kv_norm"][l], 2)
    kg = np.zeros((128, 1), np.float32)
    kg[:96, 0] = inp["k_gain"][l]
    in_maps = []
    for i in range(8):
        m = dict(maps[i])
        m.update(scan_consts_host(inp, l))
        m.update({"wukv": wukv, "kvn": kvn, "kgain": kg, "rott": rott,
                  "cosT": np.ascontiguousarray(cosT[:, i * TPC:(i + 1) * TPC]), "sinT": np.ascontiguousarray(sinT[:, i * TPC:(i + 1) * TPC])})
        in_maps.append(m)
    res = run_bass_kernel_spmd(nc, in_maps, core_ids=list(range(8)))
    return res.results


def build_lb_partial():
    nc = bass.Bass("TRN2", target_bir_lowering=False)
    P = Prog(nc)
    d = common_inputs(P)
    cons = scan_const_inputs(P)
    PRE = P.dram("PRE", [7, 3, 2, 128, 256], F32, kind="ExternalInput")
    YA = P.dram("YA", [3, 128, NTOK], F32, kind="ExternalOutput")
    U = P.dram("U", [NCH, 128, NTOK], F32)
    Ysc = P.dram("Ysc", [2, 3, 128, NTOK], F32)
    PD = P.dram("PD", [2, 3, 128, NTOK], F32)
    ps = [P.psum(f"ps{i}", [128, 512], F32) for i in range(8)]
    modt = P.sbuf("modt", [128, 48, 2]); g1t = P.sbuf("g1t", [128, 8]); flt = P.sbuf("flt", [128, NT, 2]); rwc = P.sbuf("rwc_t", [128, 12, 3])
    ones = P.sbuf("ones", [128, 128]); eps_t = P.sbuf("eps_t", [128, 1])
    for (t, s_) in ((modt, d["mod"]), (g1t, d["g1"]), (flt, d["flags"]), (rwc, d["rwc"])):
        P.dma("sp", t[:], s_[:], writes=[t])
    P.I("dve", "memset", writes=[ones], ap=ones[:], constant=1.0)
    P.I("dve", "memset", writes=[eps_t], ap=eps_t[:], constant=EPS)
    phase_proj(P, ps, d["xT"], d["cT"], modt, g1t, flt, d["win"], list(range(12)), U, rwc, ones, eps_t)
    scan_pass2(P, ps, U, cons, PRE, Ysc, PD)
    rwkv_out(P, ps, U, cons, Ysc, PD, YA)
    P.emit(final_bufs=[YA])
    print("LBp stats", P.stats())
    return nc


TWB = 128
CDEC = 0.6065306597126334
NTOK = 2304
RW_LN_EPS = 64e-5
STOP = int(os.environ.get('SCAN_STOP', '0'))
SUB = int(os.environ.get('SCAN_SUB', '9'))


class PsumPool:
    def __init__(self, P, ps):
        self.ps = ps
        self.i384 = self.i512 = self.i128 = 0

    def g384(self):
        self.i384 += 1
        b = self.ps[self.i384 % 3]
        return b, b.t[:, 0:384]

    def g512(self):
        self.i512 += 1
        b = self.ps[3 + self.i512 % 2]
        return b, b.t[:, 0:512]

    def g512h(self, h):
        b = self.ps[3 + h]
        return b, b.t[:, 0:512]

    def g128h(self, h, slot):
        b = self.ps[5 + h]
        return b, b.t[:, slot * 128:(slot + 1) * 128]

    def g128(self):
        i = self.i128
        self.i128 += 1
        b = self.ps[5 + i % 3]
        s = (i // 3) % 4
        return b, b.t[:, s * 128:(s + 1) * 128]


class ScanEnv:
    pass


def scan_setup(P, ps, cons_d, W):
    E = ScanEnv()
    E.W = W
    E.pp = PsumPool(P, ps)
    sb = P.sbuf
    E.M4 = sb("sc_M4", [128, 4, 128])
    E.I = sb("sc_I", [128, 128])
    E.BD = sb("sc_BD", [128, 128])
    E.LORA = sb("sc_LORA", [128, 2, 384])
    E.vec = sb("sc_vec", [128, 12, 3])
    for (t, k) in ((E.M4, "M4"), (E.I, "I128"), (E.BD, "BD"), (E.LORA, "LORA"), (E.vec, "rvec")):
        P.dma("sp", t[:], cons_d[k][:], reads=[cons_d[k]], writes=[t])
    E.eps12 = sb("sc_eps12", [128, 1])
    P.I("dve", "memset", writes=[E.eps12], ap=E.eps12[:], constant=1e-12)
    E.sets = []
    for i in range(2):
        S = ScanEnv()
        n = lambda s: f"sc{i}_{s}"
        S.r3 = sb(n("r3"), [128, 3, 128]); S.k3 = sb(n("k3"), [128, 3, 128]); S.v3 = sb(n("v3"), [128, 3, 128])
        S.wa = sb(n("wa"), [128, 128])
        S.kk = sb(n("kk"), [128, 3, 128]); S.sq = sb(n("sq"), [128, 3, 128]); S.rn = sb(n("rn"), [128, 3, 128])
        S.TH = sb(n("TH"), [64, 128]); S.SG = sb(n("SG"), [128, 3, 128]); S.IC = sb(n("IC"), [128, 3, 128])
        S.tmp = sb(n("tmp"), [128, 3, 128]); S.kd = sb(n("kd"), [128, 3, 128]); S.bb = sb(n("bb"), [128, 3, 128])
        S.pd = sb(n("pd"), [128, 3, 128]); S.SGT = sb(n("SGT"), [128, 3, 128])
        S.E3 = sb(n("E3"), [128, 3, 384]); S.EIn = sb(n("EIn"), [128, 3, 128])
        S.AR = sb(n("AR"), [128, 3, 2, 128]); S.BT = sb(n("BT"), [128, 3, 128]); S.KT = sb(n("KT"), [128, 3, 128])
        S.BH = sb(n("BH"), [128, 3, 128]); S.KH = sb(n("KH"), [128, 3, 128])
        S.TK = [sb(n(f"TK{p}"), [128, 3, 128]) for p in range(3)]
        S.gC = sb(n("gC"), [128, 3])
        S.hd = []
        for h in range(6):
            Hh = ScanEnv()
            for nm in ("ARB", "AAK", "ARK", "Pa", "Pb", "Qa", "Qb", "Sa", "Sb", "STa", "STb"):
                setattr(Hh, nm, sb(n(f"h{h}_{nm}"), [128, 128]))
            S.hd.append(Hh)
        S.AKV = sb(n("AKV"), [128, 3, 2, 64])
        S.X = [sb(n(f"X{p}"), [128, 2, W]) for p in range(3)]
        S.Uc = [sb(n(f"Uc{p}"), [128, 2, W]) for p in range(3)]
        S.VP = [sb(n(f"VP{p}"), [128, 2, W]) for p in range(3)]
        if W > 64:
            for p in range(3):
                P.I("pool", "memset", writes=[S.VP[p]], ap=S.VP[p][:], constant=0.0)
        S.Yt = [sb(n(f"Yt{p}"), [128, 128]) for p in range(3)]
        E.sets.append(S)
    E.H = [[sb(f"sc_H{p}{d}", [128, 2, W]) for d in range(2)] for p in range(3)]
    E.it = 0
    return E


def scan_block(P, E, U, d, col0, want_y, Ysc=None, PD=None):
    S = E.sets[E.it % 2]
    E.it += 1
    W = E.W
    pp = E.pp
    c1 = col0 + 128
    MS = E.M4[:, 0, :] if d == 0 else E.M4[:, 2, :]
    MI = E.M4[:, 1, :] if d == 0 else E.M4[:, 3, :]
    MST = E.M4[:, 2, :] if d == 0 else E.M4[:, 0, :]
    if d == 0:
        TRI = [E.M4[:, 1, :], E.M4[:, 0, :], E.M4[:, 2, :]]
    else:
        TRI = [E.M4[:, 3, :], E.M4[:, 2, :], E.M4[:, 0, :]]
    last = 127 if d == 0 else 0
    vec = E.vec
    bc3 = lambda ap: ap.unsqueeze(2).to_broadcast([128, 3, 128])
    fl3 = lambda b: b[:].rearrange("p c t -> p (c t)")
    P.dma("sp", S.r3[:], U[0:3, :, col0:c1].rearrange("c p t -> p c t"), reads=[U], writes=[S.r3])
    P.dma("sp", S.k3[:], U[3:6, :, col0:c1].rearrange("c p t -> p c t"), reads=[U], writes=[S.k3])
    P.dma("sp", S.v3[:], U[6:9, :, col0:c1].rearrange("c p t -> p c t"), reads=[U], writes=[S.v3])
    P.dma("sp", S.wa[:], U[9 + d, :, col0:c1], reads=[U], writes=[S.wa])
    P.I("dve", "tensor_tensor", reads=[S.k3, vec], writes=[S.kk], out=S.kk[:], in0=S.k3[:], in1=bc3(vec[:, 0, :]), op=ALU.mult)
    P.I("act", "activation", reads=[S.kk], writes=[S.sq], out=S.sq[:], in_=S.kk[:], func=AF.Square)
    pab, pa = pp.g384()
    P.I("pe", "matmul", reads=[E.BD, S.sq], writes=[pab], out=pa, lhsT=E.BD[:, :], rhs=fl3(S.sq), start=True, stop=True)
    P.I("act", "activation", reads=[pab, E.eps12], writes=[S.rn], out=fl3(S.rn), in_=pa, func=AF.Ln, bias=E.eps12[:, 0:1])
    P.I("act", "activation", reads=[S.rn], writes=[S.rn], out=S.rn[:], in_=S.rn[:], func=AF.Exp, scale=-0.5)
    P.I("dve", "tensor_tensor", reads=[S.kk, S.rn], writes=[S.kk], out=S.kk[:], in0=S.kk[:], in1=S.rn[:], op=ALU.mult)
    if STOP == 1:
        return
    P.I("act", "activation", reads=[S.wa], writes=[S.TH], out=S.TH[:, :], in_=S.wa[0:64, :], func=AF.Tanh)
    pzb, pz = pp.g384()
    for p in range(3):
        P.I("pe", "matmul", reads=[E.LORA, S.TH], writes=[pzb], out=pz[:, p * 128:(p + 1) * 128], lhsT=E.LORA[0:64, d, p * 128:(p + 1) * 128],
            rhs=S.TH[:, :], start=True, stop=True)
    for p in range(3):
        P.I("act", "activation", reads=[pzb, vec], writes=[S.SG], out=S.SG[:, p, :], in_=pz[:, p * 128:(p + 1) * 128], func=AF.Sigmoid,
            bias=vec[:, 4 + d, p:p + 1])
    pz2b, pz2 = pp.g384()
    for p in range(3):
        P.I("pe", "matmul", reads=[E.LORA, S.wa], writes=[pz2b], out=pz2[:, p * 128:(p + 1) * 128],
            lhsT=E.LORA[64:128, d, p * 128:(p + 1) * 128], rhs=S.wa[64:128, :], start=True, stop=True)
    for p in range(3):
        P.I("act", "activation", reads=[pz2b, vec], writes=[S.IC], out=S.IC[:, p, :], in_=pz2[:, p * 128:(p + 1) * 128], func=AF.Sigmoid,
            bias=vec[:, 6 + d, p:p + 1])
    for p in range(3):
        P.I("pool", "tensor_scalar", reads=[S.IC, vec], writes=[S.tmp], out=S.tmp[:, p, :], in0=S.IC[:, p, :], scalar1=vec[:, 1, p:p + 1],
            scalar2=vec[:, 2, p:p + 1], op0=ALU.mult, op1=ALU.add)
    P.I("dve", "tensor_tensor", reads=[S.k3, S.tmp], writes=[S.kd], out=S.kd[:], in0=S.k3[:], in1=S.tmp[:], op=ALU.mult)
    P.I("pool", "tensor_tensor", reads=[S.kk, S.IC], writes=[S.bb], out=S.bb[:], in0=S.kk[:], in1=S.IC[:], op=ALU.mult)
    if PD is not None:
        P.I("dve", "tensor_tensor", reads=[S.r3, S.kd], writes=[S.pd], out=S.pd[:], in0=S.r3[:], in1=S.kd[:], op=ALU.mult)
        P.I("pool", "tensor_tensor", reads=[S.pd, vec], writes=[S.pd], out=S.pd[:], in0=S.pd[:], in1=bc3(vec[:, 3, :]), op=ALU.mult)
        P.dma("pool", PD[d, :, :, col0:c1].rearrange("c p t -> p c t"), S.pd[:], reads=[S.pd], writes=[PD])
    if STOP == 2:
        return
    ptb, pt = pp.g384()
    for p in range(3):
        P.I("pe", "transpose", reads=[S.SG, E.I], writes=[ptb], out=pt[:, p * 128:(p + 1) * 128], in_=S.SG[:, p, :], identity=E.I[:, :])
    P.I("dve", "tensor_copy", reads=[ptb], writes=[S.SGT], out=fl3(S.SGT), in_=pt)
    for p in range(3):
        pcb, pc = pp.g384()
        for j in range(3):
            P.I("pe", "matmul", reads=[S.SGT, E.M4], writes=[pcb], out=pc[:, j * 128:(j + 1) * 128], lhsT=S.SGT[:, p, :], rhs=TRI[j],
                start=True, stop=True)
        P.I("act", "activation", reads=[pcb], writes=[S.E3], out=S.E3[:, p, :], in_=pc, func=AF.Exp, scale=-CDEC)
        P.I("act", "activation", reads=[pcb], writes=[S.EIn], out=S.EIn[:, p, :], in_=pc[:, 0:128], func=AF.Exp, scale=CDEC)
    EI = S.E3[:, :, 0:128]
    EE = S.E3[:, :, 128:256]
    ES = S.E3[:, :, 256:384]
    P.I("dve", "scalar_tensor_tensor", reads=[S.kk, S.E3], writes=[S.AR], out=S.AR[:, :, 0, :], in0=S.kk[:], scalar=-1.0, in1=EE,
        op0=ALU.mult, op1=ALU.mult)
    P.I("pool", "tensor_tensor", reads=[S.r3, S.E3], writes=[S.AR], out=S.AR[:, :, 1, :], in0=S.r3[:], in1=EI, op=ALU.mult)
    P.I("dve", "tensor_tensor", reads=[S.bb, S.EIn], writes=[S.BT], out=S.BT[:], in0=S.bb[:], in1=S.EIn[:], op=ALU.mult)
    P.I("pool", "tensor_tensor", reads=[S.kd, S.EIn], writes=[S.KT], out=S.KT[:], in0=S.kd[:], in1=S.EIn[:], op=ALU.mult)
    P.I("dve", "tensor_tensor", reads=[S.bb, S.E3], writes=[S.BH], out=S.BH[:], in0=S.bb[:], in1=ES, op=ALU.mult)
    P.I("pool", "tensor_tensor", reads=[S.kd, S.E3], writes=[S.KH], out=S.KH[:], in0=S.kd[:], in1=ES, op=ALU.mult)
    P.I("dve", "tensor_copy", reads=[S.E3], writes=[S.gC], out=S.gC[:, :], in_=S.E3[:, :, last])
    if STOP == 3:
        return
    for p in range(3):
        ptb, pt = pp.g384()
        for j, src in enumerate((S.BH, S.KH, S.v3)):
            P.I("pe", "transpose", reads=[src, E.I], writes=[ptb], out=pt[:, j * 128:(j + 1) * 128], in_=src[:, p, :], identity=E.I[:, :])
        if p % 2 == 0:
            P.I("act", "activation", reads=[ptb], writes=[S.TK[p]], out=fl3(S.TK[p]), in_=pt, func=AF.Copy)
        else:
            P.I("dve", "tensor_copy", reads=[ptb], writes=[S.TK[p]], out=fl3(S.TK[p]), in_=pt)
        if W > 64:
            P.I("pool", "tensor_copy", reads=[S.TK[p]], writes=[S.VP[p]], out=S.VP[p][:, :, 0:64],
                in_=S.TK[p][:, 2, :].rearrange("p (h c) -> p h c", c=64))
    if STOP == 4:
        return
    for hh in range(6):
        p, h = hh // 2, hh % 2
        hs = slice(h * 64, (h + 1) * 64)
        Hh = S.hd[hh]
        gb, g12 = pp.g512h(h)
        arf = S.AR[hs, p, :, :].rearrange("p a t -> p (a t)")
        P.I("pe", "matmul", reads=[S.BT, S.AR], writes=[gb], out=g12[:, 0:256], lhsT=S.BT[hs, p, :], rhs=arf, start=True, stop=True)
        P.I("pe", "matmul", reads=[S.KT, S.AR], writes=[gb], out=g12[:, 256:512], lhsT=S.KT[hs, p, :], rhs=arf, start=True, stop=True)
        P.I("dve", "tensor_tensor", reads=[gb, E.M4], writes=[Hh.Pa], out=Hh.Pa[:, :], in0=g12[:, 0:128], in1=MS, op=ALU.mult)
        P.I("dve", "tensor_tensor", reads=[gb, E.M4], writes=[Hh.ARB], out=Hh.ARB[:, :], in0=g12[:, 128:256], in1=MI, op=ALU.mult)
        P.I("dve", "tensor_tensor", reads=[gb, E.M4], writes=[Hh.AAK], out=Hh.AAK[:, :], in0=g12[:, 256:384], in1=MS, op=ALU.mult)
        P.I("dve", "tensor_tensor", reads=[gb, E.M4], writes=[Hh.ARK], out=Hh.ARK[:, :], in0=g12[:, 384:512], in1=MI, op=ALU.mult)
        P.I("pool", "tensor_tensor", reads=[Hh.Pa, E.I], writes=[Hh.Sa], out=Hh.Sa[:, :], in0=Hh.Pa[:, :], in1=E.I[:, :], op=ALU.add)
    g3s = []
    for hh in range(6):
        p, h = hh // 2, hh % 2
        hs = slice(h * 64, (h + 1) * 64)
        g3b, g3 = pp.g128h(h, p)
        P.I("pe", "matmul", reads=[S.AR, S.BT], writes=[g3b], out=g3, lhsT=S.AR[hs, p, 0, :], rhs=S.BT[hs, p, :], start=True, stop=True)
        g3s.append((g3b, g3))
    for hh in range(6):
        Hh = S.hd[hh]
        g3b, g3 = g3s[hh]
        P.I("dve", "tensor_tensor", reads=[g3b, E.M4], writes=[Hh.Qa], out=Hh.Qa[:, :], in0=g3, in1=MST, op=ALU.mult)
        P.I("pool", "tensor_tensor", reads=[Hh.Qa, E.I], writes=[Hh.STa], out=Hh.STa[:, :], in0=Hh.Qa[:, :], in1=E.I[:, :], op=ALU.add)
    if STOP == 5:
        return
    cur = {hh: dict(P=S.hd[hh].Pa, Q=S.hd[hh].Qa, S=S.hd[hh].Sa, ST=S.hd[hh].STa,
                    Pn=S.hd[hh].Pb, Qn=S.hd[hh].Qb, Sn=S.hd[hh].Sb, STn=S.hd[hh].STb) for hh in range(6)}

    def evac(bank_i, bb_, ap_, dst):
        if bank_i % 2 == 0:
            P.I("act", "activation", reads=[bb_], writes=[dst], out=dst[:, :], in_=ap_, func=AF.Copy)
        else:
            P.I("dve", "tensor_copy", reads=[bb_], writes=[dst], out=dst[:, :], in_=ap_)

    for k in range(1, 7):
        lastk = (k == 6)
        pend = []
        for hh in range(6):
            c = cur[hh]
            b1, p1 = pp.g128()
            P.I("pe", "matmul", reads=[c["Q"], c["P"]], writes=[b1], out=p1, lhsT=c["Q"][:, :], rhs=c["P"][:, :], start=True, stop=True)
            pend.append((b1, p1, c["Pn"]))
            if not lastk:
                b2, p2 = pp.g128()
                P.I("pe", "matmul", reads=[c["P"], c["Q"]], writes=[b2], out=p2, lhsT=c["P"][:, :], rhs=c["Q"][:, :], start=True, stop=True)
                pend.append((b2, p2, c["Qn"]))
        for (bb_, ap_, dst) in pend:
            evac(E.pp.ps.index(bb_), bb_, ap_, dst)
        pend = []
        for hh in range(6):
            c = cur[hh]
            b3, p3 = pp.g128()
            P.I("pe", "matmul", reads=[c["ST"], c["Pn"]], writes=[b3], out=p3, lhsT=c["ST"][:, :], rhs=c["Pn"][:, :], start=True, stop=False)
            P.I("pe", "matmul", reads=[E.I, c["S"]], writes=[b3], out=p3, lhsT=E.I[:, :], rhs=c["S"][:, :], start=False, stop=True)
            pend.append((b3, p3, c["Sn"]))
            if not lastk:
                b4, p4 = pp.g128()
                P.I("pe", "matmul", reads=[c["Pn"], c["ST"]], writes=[b4], out=p4, lhsT=c["Pn"][:, :], rhs=c["ST"][:, :], start=True, stop=False)
                P.I("pe", "matmul", reads=[E.I, c["ST"]], writes=[b4], out=p4, lhsT=E.I[:, :], rhs=c["ST"][:, :], start=False, stop=True)
                pend.append((b4, p4, c["STn"]))
        for (bb_, ap_, dst) in pend:
            evac(E.pp.ps.index(bb_), bb_, ap_, dst)
        for hh in range(6):
            c = cur[hh]
            c["P"], c["Pn"] = c["Pn"], c["P"]
            c["Q"], c["Qn"] = c["Qn"], c["Q"]
            c["S"], c["Sn"] = c["Sn"], c["S"]
            c["ST"], c["STn"] = c["STn"], c["ST"]
    TT = {hh: cur[hh]["S"] for hh in range(6)}
    if STOP == 6:
        return
    pend = []
    for p in range(3):
        pkb, pk = pp.g128()
        for h in range(2):
            hh = 2 * p + h
            P.I("pe", "matmul", reads=[S.hd[hh].AAK, S.TK[p]], writes=[pkb], out=pk[:, h * 64:(h + 1) * 64], lhsT=S.hd[hh].AAK[:, :],
                rhs=S.TK[p][:, 2, h * 64:(h + 1) * 64], start=True, stop=True)
        pend.append((pkb, pk, p))
    for (pkb, pk, p) in pend:
        P.I("act", "activation", reads=[pkb], writes=[S.AKV], out=S.AKV[:, p, :, :], in_=pk.rearrange("p (h c) -> p h c", c=64), func=AF.Copy)
    if STOP == 7:
        return
    flh = lambda b: b[:].rearrange("p h c -> p (h c)")
    for p in range(3):
        H = E.H[p][d]
        vp_ap = flh(S.VP[p]) if W > 64 else S.TK[p][:, 2, :]
        vp_buf = S.VP[p] if W > 64 else S.TK[p]
        b1, p1 = pp.g512()
        P.I("pe", "matmul", reads=[S.AR, H], writes=[b1], out=p1[:, 0:2 * W], lhsT=S.AR[:, p, 0, :], rhs=flh(H), start=True, stop=True)
        p1v = p1[:, 0:2 * W].rearrange("p (h c) -> p h c", c=W)
        P.I("dve", "tensor_tensor", reads=[b1, S.AKV], writes=[S.X[p]], out=S.X[p][:, :, 0:64], in0=p1v[:, :, 0:64], in1=S.AKV[:, p, :, :],
            op=ALU.add)
        if W > 64:
            P.I("dve", "tensor_copy", reads=[b1], writes=[S.X[p]], out=S.X[p][:, :, 64:W], in_=p1v[:, :, 64:W])
        b2, p2 = pp.g512()
        for h in range(2):
            P.I("pe", "matmul", reads=[TT[2 * p + h], S.X[p]], writes=[b2], out=p2[:, h * W:(h + 1) * W], lhsT=TT[2 * p + h][:, :],
                rhs=S.X[p][:, h, :], start=True, stop=True)
        P.I("act", "activation", reads=[b2], writes=[S.Uc[p]], out=flh(S.Uc[p]), in_=p2[:, 0:2 * W], func=AF.Copy)
        if want_y:
            for h in range(2):
                hh = 2 * p + h
                hs = slice(h * 64, (h + 1) * 64)
                pyb, py = pp.g128()
                P.I("pe", "matmul", reads=[H, S.AR], writes=[pyb], out=py, lhsT=flh(H), rhs=S.AR[:, p, 1, :], start=True, stop=False)
                P.I("pe", "matmul", reads=[S.Uc[p], S.hd[hh].ARB], writes=[pyb], out=py, lhsT=flh(S.Uc[p]), rhs=S.hd[hh].ARB[:, :],
                    start=False, stop=False)
                P.I("pe", "matmul", reads=[S.TK[p], S.hd[hh].ARK], writes=[pyb], out=py, lhsT=S.TK[p][:, 2, :], rhs=S.hd[hh].ARK[:, :],
                    start=False, stop=True)
                if h == 0:
                    P.I("act", "activation", reads=[pyb], writes=[S.Yt[p]], out=S.Yt[p][hs, :], in_=py[hs, :], func=AF.Copy)
                else:
                    P.I("dve", "tensor_copy", reads=[pyb], writes=[S.Yt[p]], out=S.Yt[p][hs, :], in_=py[hs, :])
            P.dma("pool", Ysc[d, p, :, col0:c1], S.Yt[p][:, :], reads=[S.Yt[p]], writes=[Ysc])
        b3, p3 = pp.g512()
        P.I("pe", "matmul", reads=[S.TK[p], S.Uc[p]], writes=[b3], out=p3[:, 0:2 * W], lhsT=S.TK[p][:, 0, :], rhs=flh(S.Uc[p]),
            start=True, stop=False)
        P.I("pe", "matmul", reads=[S.TK[p], vp_buf], writes=[b3], out=p3[:, 0:2 * W], lhsT=S.TK[p][:, 1, :], rhs=vp_ap, start=False, stop=True)
        for h in range(2):
            hs = slice(h * 64, (h + 1) * 64)
            P.I("dve", "scalar_tensor_tensor", reads=[H, S.gC, b3], writes=[H], out=H[hs, h, :], in0=H[hs, h, :], scalar=S.gC[hs, p:p + 1],
                in1=p3[hs, h * W:(h + 1) * W], op0=ALU.mult, op1=ALU.add)


def scan_consts_host(inp, l):
    r = np.arange(128)
    LT = (r[:, None] < r[None, :]).astype(np.float32)
    LE = (r[:, None] <= r[None, :]).astype(np.float32)
    M4 = np.stack([LT, LE, LT.T, LE.T], axis=1)
    I = np.eye(128, dtype=np.float32)
    BD = np.zeros((128, 128), np.float32)
    BD[:64, :64] = 1
    BD[64:, 64:] = 1
    LORA = np.zeros((128, 2, 384), np.float32)
    for d in range(2):
        LORA[:64, d] = inp["rw_w_up"][l][d]
        LORA[64:, d] = inp["rw_a_up"][l][d]
    vec = np.zeros((128, 12, 3), np.float32)
    v3 = lambda v: v.reshape(3, 128).T
    vec[:, 0] = v3(inp["rw_k_k"][l])
    vec[:, 1] = v3(inp["rw_k_a"][l])
    vec[:, 2] = v3(1.0 - inp["rw_k_a"][l])
    vec[:, 3] = v3(inp["rw_r_k"][l].reshape(-1))
    vec[:, 4] = v3(inp["rw_w0"][l][0]); vec[:, 5] = v3(inp["rw_w0"][l][1])
    vec[:, 6] = v3(inp["rw_a0"][l][0]); vec[:, 7] = v3(inp["rw_a0"][l][1])
    vec[:, 8] = v3(inp["rw_ln_w"][l]); vec[:, 9] = v3(inp["rw_ln_b"][l])
    return dict(M4=np.ascontiguousarray(M4), I128=I, BD=BD, LORA=LORA, rvec=np.ascontiguousarray(vec),
                gup=np.ascontiguousarray(inp["rw_g_up"][l]))


def scan_const_inputs(P):
    di = lambda name, shape, dt=F32: P.dram(name, shape, dt, kind="ExternalInput")
    return dict(M4=di("M4", [128, 4, 128]), I128=di("I128", [128, 128]), BD=di("BD", [128, 128]), LORA=di("LORA", [128, 2, 384]),
                rvec=di("rvec", [128, 12, 3]), gup=di("gup", [128, 384]))


def scan_pass1(P, ps, U, cons_d, seg_out):
    with P.scope():
        E = scan_setup(P, ps, cons_d, 128)
        for p in range(3):
            for d in range(2):
                H = E.H[p][d]
                P.I("pool", "memset", writes=[H], ap=H[:], constant=0.0)
                for h in range(2):
                    hs = slice(h * 64, (h + 1) * 64)
                    P.I("pool", "tensor_copy", reads=[E.I, H], writes=[H], out=H[hs, h, 64:128], in_=E.I[hs, h * 64:(h + 1) * 64])
        for i in range(16):
            scan_block(P, E, U, 0, i * 128, False)
            scan_block(P, E, U, 1, (15 - i) * 128, False)
        for p in range(3):
            for d in range(2):
                P.dma("sp", seg_out[p, d, :, :], E.H[p][d][:].rearrange("p h c -> p (h c)"), reads=[E.H[p][d]], writes=[seg_out])


def scan_pass2(P, ps, U, cons_d, PRE, Ysc, PD):
    with P.scope():
        E = scan_setup(P, ps, cons_d, 64)
        for p in range(3):
            for d in range(2):
                P.I("pool", "memset", writes=[E.H[p][d]], ap=E.H[p][d][:], constant=0.0)
        for i in range(2):
            scan_block(P, E, U, 0, 2048 + i * 128, True, Ysc, PD)
            scan_block(P, E, U, 1, 2048 + (1 - i) * 128, True, Ysc, PD)
        pre = [P.sbuf(f"pre{i}", [128, 256]) for i in range(2)]
        n = 0
        for j in range(7):
            for p in range(3):
                for d in range(2):
                    pb = pre[n % 2]
                    n += 1
                    P.dma("sp", pb[:], PRE[j, p, d, :, :], reads=[PRE], writes=[pb])
                    H = E.H[p][d]
                    pqb, pq = E.pp.g128()
                    P.I("pe", "matmul", reads=[pb, H], writes=[pqb], out=pq, lhsT=pb[:, 0:128], rhs=H[:].rearrange("p h c -> p (h c)"),
                        start=True, stop=True)
                    P.I("dve", "tensor_tensor", reads=[pqb, pb], writes=[H], out=H[:].rearrange("p h c -> p (h c)"), in0=pq,
                        in1=pb[:, 128:256], op=ALU.add)
        for i in range(16):
            scan_block(P, E, U, 0, i * 128, True, Ysc, PD)
            scan_block(P, E, U, 1, (15 - i) * 128, True, Ysc, PD)


def rwkv_out(P, ps, U, cons_d, Ysc, PD, YA):
    with P.scope():
        BD = P.sbuf("ro_BD", [128, 128]); vec = P.sbuf("ro_vec", [128, 12, 3]); gup = P.sbuf("ro_gup", [128, 384])
        P.dma("sp", BD[:], cons_d["BD"][:], writes=[BD])
        P.dma("sp", vec[:], cons_d["rvec"][:], writes=[vec])
        P.dma("sp", gup[:], cons_d["gup"][:], writes=[gup])
        epsl = P.sbuf("ro_eps", [128, 1])
        P.I("dve", "memset", writes=[epsl], ap=epsl[:], constant=RW_LN_EPS)
        TWL = 256
        mk = lambda nm, shape=[128, TWL]: [P.sbuf(f"ro_{nm}{i}", shape) for i in range(2)]
        y0, y1, yc, sq, rs, pd0, pd1, vv, gd, sg, ot = (mk(s) for s in ("y0", "y1", "yc", "sq", "rs", "pd0", "pd1", "vv", "gd", "sg", "ot"))
        it = 0
        for ti in range(9):
            c0, c1 = ti * TWL, (ti + 1) * TWL
            g_ = gd[ti % 2]
            s_ = sg[ti % 2]
            P.dma("sp", g_[:], U[11, :, c0:c1], reads=[U], writes=[g_])
            P.I("act", "activation", reads=[g_], writes=[s_], out=s_[:], in_=g_[:], func=AF.Sigmoid)
            for p in range(3):
                i = it % 2
                it += 1
                P.dma("sp", y0[i][:], Ysc[0, p, :, c0:c1], reads=[Ysc], writes=[y0[i]])
                P.dma("sp", y1[i][:], Ysc[1, p, :, c0:c1], reads=[Ysc], writes=[y1[i]])
                P.dma("sp", pd0[i][:], PD[0, p, :, c0:c1], reads=[PD], writes=[pd0[i]])
                P.dma("sp", pd1[i][:], PD[1, p, :, c0:c1], reads=[PD], writes=[pd1[i]])
                P.dma("sp", vv[i][:], U[6 + p, :, c0:c1], reads=[U], writes=[vv[i]])
                P.I("dve", "tensor_tensor", reads=[y0[i], y1[i]], writes=[y0[i]], out=y0[i][:], in0=y0[i][:], in1=y1[i][:], op=ALU.add)
                pm = ps[0]
                P.I("pe", "matmul", reads=[BD, y0[i]], writes=[pm], out=pm[:, 0:TWL], lhsT=BD[:, :], rhs=y0[i][:], start=True, stop=True)
                P.I("dve", "scalar_tensor_tensor", reads=[pm, y0[i]], writes=[yc[i]], out=yc[i][:], in0=pm[:, 0:TWL], scalar=-1.0 / 64,
                    in1=y0[i][:], op0=ALU.mult, op1=ALU.add)
                P.I("act", "activation", reads=[yc[i]], writes=[sq[i]], out=sq[i][:], in_=yc[i][:], func=AF.Square)
                pv = ps[1]
                P.I("pe", "matmul", reads=[BD, sq[i]], writes=[pv], out=pv[:, 0:TWL], lhsT=BD[:, :], rhs=sq[i][:], start=True, stop=True)
                P.I("act", "activation", reads=[pv, epsl], writes=[rs[i]], out=rs[i][:], in_=pv[:, 0:TWL], func=AF.Ln, scale=1.0 / 64,
                    bias=epsl[:, 0:1])
                P.I("act", "activation", reads=[rs[i]], writes=[rs[i]], out=rs[i][:], in_=rs[i][:], func=AF.Exp, scale=-0.5)
                P.I("dve", "tensor_tensor", reads=[yc[i], rs[i]], writes=[yc[i]], out=yc[i][:], in0=yc[i][:], in1=rs[i][:], op=ALU.mult)
                P.I("pool", "tensor_scalar", reads=[yc[i], vec], writes=[yc[i]], out=yc[i][:], in0=yc[i][:], scalar1=vec[:, 8, p:p + 1],
                    scalar2=vec[:, 9, p:p + 1], op0=ALU.mult, op1=ALU.add)
                P.I("pool", "tensor_tensor", reads=[pd0[i], pd1[i]], writes=[pd0[i]], out=pd0[i][:], in0=pd0[i][:], in1=pd1[i][:], op=ALU.add)
                pb = ps[2]
                P.I("pe", "matmul", reads=[BD, pd0[i]], writes=[pb], out=pb[:, 0:TWL], lhsT=BD[:, :], rhs=pd0[i][:], start=True, stop=True)
                P.I("dve", "tensor_tensor", reads=[pb, vv[i]], writes=[vv[i]], out=vv[i][:], in0=pb[:, 0:TWL], in1=vv[i][:], op=ALU.mult)
                P.I("pool", "tensor_tensor", reads=[yc[i], vv[i]], writes=[yc[i]], out=yc[i][:], in0=yc[i][:], in1=vv[i][:], op=ALU.add)
                pg = ps[3]
                P.I("pe", "matmul", reads=[gup, s_], writes=[pg], out=pg[:, 0:TWL], lhsT=gup[:, p * 128:(p + 1) * 128], rhs=s_[:],
                    start=True, stop=True)
                P.I("dve", "tensor_tensor", reads=[pg, yc[i]], writes=[ot[i]], out=ot[i][:], in0=pg[:, 0:TWL], in1=yc[i][:], op=ALU.mult)
                P.dma("pool", YA[p, :, c0:c1], ot[i][:], reads=[ot[i]], writes=[YA])


def make_pre(segs, core):
    PRE = np.zeros((7, 3, 2, 128, 256), np.float32)
    PRE[:, :, :, :, 0:128] = np.eye(128, dtype=np.float32)
    for d in range(2):
        order = list(range(0, core)) if d == 0 else list(range(7, core, -1))
        for j, c in enumerate(order):
            for p in range(3):
                s = segs[c][p, d]
                blk = np.zeros((128, 256), np.float32)
                for h in range(2):
                    hs = slice(h * 64, (h + 1) * 64)
                    Hz = s[hs, h * 128: h * 128 + 64]
                    Pm = s[hs, h * 128 + 64: h * 128 + 128]
                    blk[hs, hs] = Pm.T
                    blk[hs, 128 + h * 64: 128 + (h + 1) * 64] = Hz
                PRE[j, p, d] = blk
    return PRE


NKT = 130
NKEY = NKT * 128
ATT_SCALE = 96 ** -0.5


def phase_q(P, ps, U, qn_t, wuq_d, qgain_t, rott, cos_d, sin_d, ones, eps_t, QT):
    with P.scope():
        wq = P.sbuf("wuq_sb", [128, 3, 576])
        P.dma("sp", wq[:], wuq_d[:], reads=[wuq_d], writes=[wq])
        u = P.sbuf("q_u", [128, 3, TW]); sq = P.sbuf("q_sq", [128, 3, TW]); rstd = P.sbuf("q_rstd", [128, TW]); cq = P.sbuf("q_cq", [128, 3, TW])
        qc = [P.sbuf(f"q_qc{i}", [96, TW]) for i in range(2)]
        qsq = P.sbuf("q_qsq", [96, 1, TW]); qrstd = P.sbuf("q_qrstd", [96, TW]); qn = P.sbuf("q_qn", [96, TW]); t1 = P.sbuf("q_t1", [96, TW])
        cs = P.sbuf("q_cs", [96, 2, TW])
        qb = [P.sbuf(f"q_qb{i}", [96, TW], BF16) for i in range(2)]
        for ti in range(NT):
            for c in range(3):
                P.dma("sp", u[:, c, :], U[12 + c, :, ti * TW:(ti + 1) * TW], reads=[U], writes=[u])
            rms_bcast(P, ps[7], ones[:, :], sq, (lambda k: u[:, k, :], u), rstd, 3, TW, 384.0, eps_t, ones)
            for k in range(3):
                P.I("dve", "scalar_tensor_tensor", reads=[u, qn_t, rstd], writes=[cq], out=cq[:, k, :], in0=u[:, k, :],
                    scalar=qn_t[:, k:k + 1], in1=rstd[:, :], op0=ALU.mult, op1=ALU.mult)
            if ti < 8:
                P.dma("sp", cs[:, 0, :], cos_d[:, ti * TW:(ti + 1) * TW], reads=[cos_d], writes=[cs])
                P.dma("sp", cs[:, 1, :], sin_d[:, ti * TW:(ti + 1) * TW], reads=[sin_d], writes=[cs])
            for hh in range(6):
                pk = ps[hh % 2]
                for k in range(3):
                    P.I("pe", "matmul", reads=[cq, wq], writes=[pk], out=pk[0:96, 0:TW], lhsT=wq[:, k, hh * 96:(hh + 1) * 96],
                        rhs=cq[:, k, :], start=(k == 0), stop=(k == 2))
                qc_ = qc[hh % 2]
                P.I("act", "activation", reads=[pk], writes=[qc_], out=qc_[:, :], in_=pk[0:96, 0:TW], func=AF.Copy)
                rms_bcast(P, ps[6], ones[0:96, 0:96], qsq, (lambda k, qc_=qc_: qc_[:, :], qc_), qrstd, 1, TW, 96.0, eps_t, ones)
                P.I("dve", "scalar_tensor_tensor", reads=[qc_, qgain_t, qrstd], writes=[qn], out=qn[:, :], in0=qc_[:, :],
                    scalar=qgain_t[0:96, 0:1], in1=qrstd[0:96, :], op0=ALU.mult, op1=ALU.mult)
                qb_ = qb[hh % 2]
                if ti < 8:
                    pr = ps[2 + hh % 2]
                    P.I("pe", "matmul", reads=[rott, qn], writes=[pr], out=pr[0:96, 0:TW], lhsT=rott[0:96, 0:96], rhs=qn[:, :],
                        start=True, stop=True)
                    P.I("dve", "tensor_tensor", reads=[pr, cs], writes=[t1], out=t1[:, :], in0=pr[0:96, 0:TW], in1=cs[0:96, 1, :], op=ALU.mult)
                    P.I("pool", "tensor_tensor", reads=[qn, cs], writes=[qn], out=qn[:, :], in0=qn[:, :], in1=cs[0:96, 0, :], op=ALU.mult)
                    P.I("dve", "tensor_tensor", reads=[qn, t1], writes=[qb_], out=qb_[:, :], in0=qn[:, :], in1=t1[:, :], op=ALU.add)
                else:
                    P.I("dve", "tensor_copy", reads=[qn], writes=[qb_], out=qb_[:, :], in_=qn[:, :])
                P.dma("pool", QT[hh, :, ti * TW:(ti + 1) * TW], qb_[:, :], reads=[qb_], writes=[QT])


def phase_attn(P, ps, QT, KT, VH, sel_d, YB):
    with P.scope():
        kb = [P.sbuf(f"at_k{i}", [96, NKEY], BF16) for i in range(2)]
        vb = [P.sbuf(f"at_v{i}", [128, NKT, 65], BF16) for i in range(2)]
        qb = [P.sbuf(f"at_q{i}", [96, NTOK], BF16) for i in range(2)]
        sel = P.sbuf("at_sel", [65, 64])
        P.dma("sp", sel[:], sel_d[:], writes=[sel])
        pT = [P.sbuf(f"at_p{i}", [128, 512], BF16) for i in range(4)]
        accs = P.sbuf("at_accs", [65, 512]); rec = P.sbuf("at_rec", [64, 512]); ob = [P.sbuf(f"at_o{i}", [64, 512]) for i in range(2)]
        n = 0
        nq = 0
        for hh in range(6):
            k_, v_, q_ = kb[hh % 2], vb[hh % 2], qb[hh % 2]
            for c4 in range(4):
                a, b = c4 * (NKEY // 4), (c4 + 1) * (NKEY // 4)
                P.dma("sp", k_[:, a:b], KT[hh, :, a:b], reads=[KT], writes=[k_])
            P.dma("sp", v_[:], VH[hh], reads=[VH], writes=[v_])
            P.dma("sp", q_[:], QT[hh], reads=[QT], writes=[q_])
            for qt in range(5):
                if qt < 4:
                    q0, qw, kts = qt * 512, 512, range(NKT)
                else:
                    q0, qw, kts = 2048, 256, range(2)
                acc = ps[3 + nq % 2]
                nq += 1
                kts = list(kts)
                for i, kt in enumerate(kts):
                    pss = ps[n % 3]
                    pt_ = pT[n % 4]
                    n += 1
                    P.I("pe", "matmul", reads=[k_, q_], writes=[pss], out=pss[:, 0:qw], lhsT=k_[:, kt * 128:(kt + 1) * 128], rhs=q_[:, q0:q0 + qw],
                        start=True, stop=True)
                    P.I("act", "activation", reads=[pss], writes=[pt_], out=pt_[:, 0:qw], in_=pss[:, 0:qw], func=AF.Exp, scale=ATT_SCALE)
                    P.I("pe", "matmul", reads=[v_, pt_], writes=[acc], out=acc[0:65, 0:qw], lhsT=v_[:, kt, :], rhs=pt_[:, 0:qw],
                        start=(i == 0), stop=(i == len(kts) - 1))
                P.I("dve", "tensor_copy", reads=[acc], writes=[accs], out=accs[:, 0:qw], in_=acc[0:65, 0:qw])
                pd_ = ps[5]
                P.I("pe", "matmul", reads=[sel, accs], writes=[pd_], out=pd_[0:64, 0:qw], lhsT=sel[:, :], rhs=accs[:, 0:qw], start=True, stop=True)
                P.I("dve", "reciprocal", reads=[pd_], writes=[rec], out=rec[:, 0:qw], in_=pd_[0:64, 0:qw])
                o_ = ob[nq % 2]
                P.I("dve", "tensor_tensor", reads=[accs, rec], writes=[o_], out=o_[:, 0:qw], in0=accs[0:64, 0:qw], in1=rec[:, 0:qw], op=ALU.mult)
                P.dma("pool", YB[hh // 2, (hh % 2) * 64:(hh % 2) * 64 + 64, q0:q0 + qw], o_[:, 0:qw], reads=[o_], writes=[YB])


def phase_merge(P, ps, U, YA, YB, PCS, xT, cT, modt, w_d, cmat_d, xm_out):
    with P.scope():
        stage = [P.sbuf(f"mg_st{i}", [128, 1024]) for i in range(2)]
        wb = {}
        i = 0
        for nm, nk in (("wba", 3), ("wbb", 3), ("wbc", 2), ("wout", 8)):
            wb[nm] = P.sbuf(f"mg_{nm}", [128, nk, 1024], BF16)
            for k in range(nk):
                st = stage[i % 2]
                P.dma("sp", st[:], w_d[nm][:, k, :], reads=[w_d[nm]], writes=[st])
                if i % 2 == 0:
                    P.I("pool", "tensor_copy", reads=[st], writes=[wb[nm]], out=wb[nm][:, k, :], in_=st[:])
                else:
                    P.I("act", "activation", reads=[st], writes=[wb[nm]], out=wb[nm][:, k, :], in_=st[:], func=AF.Copy)
                i += 1
        cm = P.sbuf("mg_cm", [128, 2, 128])
        P.dma("sp", cm[:], cmat_d[:], writes=[cm])
        yin = P.sbuf("mg_yin", [128, 6, TW]); pin = P.sbuf("mg_pin", [128, 2, 2, TW])
        ya = P.sbuf("mg_ya", [128, 3, TW], BF16); yb = P.sbuf("mg_yb", [128, 3, TW], BF16); yc = P.sbuf("mg_yc", [128, 2, TW], BF16)
        g3 = [P.sbuf(f"mg_g{i}", [128, 3, TW]) for i in range(2)]
        t3 = [P.sbuf(f"mg_t{i}", [128, 3, TW]) for i in range(2)]
        mb = P.sbuf("mg_mb", [128, 8, TW], BF16)
        xt = P.sbuf("mg_x", [128, 8, TW]); xo = P.sbuf("mg_xo", [128, 8, TW])
        for ti in range(NT):
            j = 0 if ti < 8 else 1
            c0, c1 = ti * TW, (ti + 1) * TW
            P.dma("sp", yin[:, 0:3, :], YA[:, :, c0:c1].rearrange("c p t -> p c t"), reads=[YA], writes=[yin])
            P.dma("sp", yin[:, 3:6, :], YB[:, :, c0:c1].rearrange("c p t -> p c t"), reads=[YB], writes=[yin])
            for a in range(2):
                P.dma("sp", pin[:, a, :, :], PCS[a, :, :, c0:c1].rearrange("c p t -> p c t"), reads=[PCS], writes=[pin])
            if ti < 8:
                src = xT[:].rearrange("(k p) t -> p k t", p=128)[:, :, 1 + c0: 1 + c1]
            else:
                src = cT[:].rearrange("(k p) t -> p k t", p=128)[:, :, 1:1 + TW]
            P.dma("sp", xt[:], src, reads=[xT, cT], writes=[xt])
            P.I("pool", "tensor_copy", reads=[yin], writes=[ya], out=ya[:], in_=yin[:, 0:3, :])
            P.I("pool", "tensor_copy", reads=[yin], writes=[yb], out=yb[:], in_=yin[:, 3:6, :])
            for c in range(2):
                pc = ps[6]
                P.I("pe", "matmul", reads=[cm, pin], writes=[pc], out=pc[:, 0:TW], lhsT=cm[:, 0, :], rhs=pin[:, 0, c, :], start=True, stop=False)
                P.I("pe", "matmul", reads=[cm, pin], writes=[pc], out=pc[:, 0:TW], lhsT=cm[:, 1, :], rhs=pin[:, 1, c, :], start=False, stop=True)
                P.I("act", "activation", reads=[pc], writes=[yc], out=yc[:, c, :], in_=pc[:, 0:TW], func=AF.Copy)
            for m in range(8):
                g_ = g3[m % 2]
                t_ = t3[m % 2]
                for bi in range(3):
                    P.dma("pool", g_[:, bi, :], U[20 + 8 * bi + m, :, c0:c1], reads=[U], writes=[g_])
                P.I("act", "activation", reads=[g_], writes=[g_], out=g_[:], in_=g_[:], func=AF.Sigmoid)
                for bi, (nm, yy, nk) in enumerate((("wba", ya, 3), ("wbb", yb, 3), ("wbc", yc, 2))):
                    pp_ = ps[bi]
                    for k in range(nk):
                        P.I("pe", "matmul", reads=[wb[nm], yy], writes=[pp_], out=pp_[:, 0:TW], lhsT=wb[nm][:, k, m * 128:(m + 1) * 128],
                            rhs=yy[:, k, :], start=(k == 0), stop=(k == nk - 1))
                    P.I("dve", "tensor_tensor", reads=[pp_, g_], writes=[t_], out=t_[:, bi, :], in0=pp_[:, 0:TW], in1=g_[:, bi, :], op=ALU.mult)
                P.I("pool", "tensor_tensor", reads=[t_], writes=[t_], out=t_[:, 0, :], in0=t_[:, 0, :], in1=t_[:, 1, :], op=ALU.add)
                P.I("pool", "tensor_tensor", reads=[t_], writes=[mb], out=mb[:, m, :], in0=t_[:, 0, :], in1=t_[:, 2, :], op=ALU.add)
            for m in range(8):
                po = ps[3 + m % 2]
                for k in range(8):
                    P.I("pe", "matmul", reads=[wb["wout"], mb], writes=[po], out=po[:, 0:TW], lhsT=wb["wout"][:, k, m * 128:(m + 1) * 128],
                        rhs=mb[:, k, :], start=(k == 0), stop=(k == 7))
                P.I("dve", "scalar_tensor_tensor", reads=[po, modt, xt], writes=[xo], out=xo[:, m, :], in0=po[:, 0:TW],
                    scalar=modt[:, 16 + m, j:j + 1], in1=xt[:, m, :], op0=ALU.mult, op1=ALU.add)
            P.dma("sp", xm_out[:].rearrange("(k p) t -> p k t", p=128)[:, :, c0:c1], xo[:], reads=[xo], writes=[xm_out])


LSEQ = 16384
LCTX = 256
CPC = 32


def fft_stage(P, ps, X, n1, cs1, tw, c2s2, Y, T1, T2, out_d1, out_d2, pfx):
    w = 2 * n1
    per_bank = max(1, 512 // w)
    nb = 0
    c = 0
    Yv = Y[:, :, 0:w]
    while c < CPC:
        g = min(per_bank, CPC - c)
        pp = ps[nb % 2]
        for i in range(g):
            P.I("pe", "matmul", reads=[X, cs1], writes=[pp], out=pp[:, i * w:(i + 1) * w], lhsT=X[0:n1, c + i, :], rhs=cs1[0:n1, 0:w],
                start=True, stop=True)
        eng = "act" if nb % 2 == 0 else "dve"
        if eng == "act":
            P.I("act", "activation", reads=[pp], writes=[Y], out=Y[:, c:c + g, 0:w], in_=pp[:, 0:g * w].rearrange("p (c w) -> p c w", w=w),
                func=AF.Copy)
        else:
            P.I("dve", "tensor_copy", reads=[pp], writes=[Y], out=Y[:, c:c + g, 0:w], in_=pp[:, 0:g * w].rearrange("p (c w) -> p c w", w=w))
        c += g
        nb += 1
    Yr = Y[:, :, 0:n1]
    Ys = Y[:, :, n1:2 * n1]
    Tc = tw[:, 0, 0:n1].unsqueeze(1).to_broadcast([128, CPC, n1])
    Ts = tw[:, 1, 0:n1].unsqueeze(1).to_broadcast([128, CPC, n1])
    A = T2[:, :, 0:n1]
    B = T2[:, :, n1:2 * n1]
    P.I("dve", "tensor_tensor", reads=[Y, tw], writes=[T2], out=A, in0=Yr, in1=Tc, op=ALU.mult)
    P.I("pool", "tensor_tensor", reads=[Y, tw], writes=[T1], out=T1[:, :, 0:n1], in0=Ys, in1=Ts, op=ALU.mult)
    P.I("dve", "tensor_tensor", reads=[T1, T2], writes=[T1], out=T1[:, :, 0:n1], in0=A, in1=T1[:, :, 0:n1], op=ALU.subtract)
    P.I("pool", "tensor_tensor", reads=[Y, tw], writes=[T2], out=B, in0=Ys, in1=Tc, op=ALU.mult)
    P.I("dve", "tensor_tensor", reads=[Y, tw], writes=[T1], out=T1[:, :, n1:2 * n1], in0=Yr, in1=Ts, op=ALU.mult)
    P.I("dve", "tensor_tensor", reads=[T1, T2], writes=[T1], out=T1[:, :, n1:2 * n1], in0=B, in1=T1[:, :, n1:2 * n1], op=ALU.add)
    cpb = max(1, 512 // n1)
    c = 0
    nb = 0
    O1 = Y[:, :, 0:n1]
    O2 = Y[:, :, n1:2 * n1]
    while c < CPC:
        g = min(cpb, CPC - c)
        p1, p2 = ps[2 + (nb % 2)], ps[4 + (nb % 2)]
        yr = T1[:, c:c + g, 0:n1]
        ys = T1[:, c:c + g, n1:2 * n1]
        P.I("pe", "matmul", reads=[c2s2, T1], writes=[p1], out=p1[:, 0:g * n1], lhsT=c2s2[:, 0, :], rhs=yr, start=True, stop=False)
        P.I("pe", "matmul", reads=[c2s2, T1], writes=[p1], out=p1[:, 0:g * n1], lhsT=c2s2[:, 2, :], rhs=ys, start=False, stop=True)
        P.I("pe", "matmul", reads=[c2s2, T1], writes=[p2], out=p2[:, 0:g * n1], lhsT=c2s2[:, 0, :], rhs=ys, start=True, stop=False)
        P.I("pe", "matmul", reads=[c2s2, T1], writes=[p2], out=p2[:, 0:g * n1], lhsT=c2s2[:, 1, :], rhs=yr, start=False, stop=True)
        P.I("act", "activation", reads=[p1], writes=[Y], out=O1[:, c:c + g, :], in_=p1[:, 0:g * n1].rearrange("p (c w) -> p c w", w=n1),
            func=AF.Copy)
        P.I("dve", "tensor_copy", reads=[p2], writes=[Y], out=O2[:, c:c + g, :], in_=p2[:, 0:g * n1].rearrange("p (c w) -> p c w", w=n1))
        c += g
        nb += 1
    P.dma("sp", out_d1[:].rearrange("c (a b) -> a c b", b=n1), O1, reads=[Y], writes=[out_d1], allow_slow_non_contiguous=True)
    P.dma("sp", out_d2[:].rearrange("c (a b) -> a c b", b=n1), O2, reads=[Y], writes=[out_d2], allow_slow_non_contiguous=True)


def build_lf():
    nc = bass.Bass("TRN2", target_bir_lowering=False)
    P = Prog(nc)
    di = lambda name, shape, dt=F32: P.dram(name, shape, dt, kind="ExternalInput")
    fl = di("fl", [CPC, LSEQ])
    fc = di("fc", [CPC, LCTX])
    cs1_d = di("cs1", [128, 256])
    tw_d = di("tw", [128, 2, 128])
    c2s2_d = di("c2s2", [128, 3, 128])
    cs1c_d = di("cs1c", [2, 4])
    twc_d = di("twc", [128, 2, 2])
    o1 = P.dram("p1", [CPC, LSEQ], F32, kind="ExternalOutput")
    o2 = P.dram("p2", [CPC, LSEQ], F32, kind="ExternalOutput")
    o1c = P.dram("p1c", [CPC, LCTX], F32, kind="ExternalOutput")
    o2c = P.dram("p2c", [CPC, LCTX], F32, kind="ExternalOutput")
    ps = [P.psum(f"ps{i}", [128, 512], F32) for i in range(8)]
    cs1 = P.sbuf("cs1_t", [128, 256])
    tw = P.sbuf("tw_t", [128, 2, 128])
    c2s2 = P.sbuf("c2s2_t", [128, 3, 128])
    cs1c = P.sbuf("cs1c_t", [2, 4])
    twc = P.sbuf("twc_t", [128, 2, 2])
    for (t, s) in ((cs1, cs1_d), (tw, tw_d), (c2s2, c2s2_d), (cs1c, cs1c_d), (twc, twc_d)):
        P.dma("sp", t[:], s[:], writes=[t])
    X = P.sbuf("X", [128, CPC, 128])
    Y = P.sbuf("Y", [128, CPC, 256])
    T1 = P.sbuf("T1", [128, CPC, 256])
    T2 = P.sbuf("T2", [128, CPC, 256])
    P.dma("sp", X[:], fl[:].rearrange("c (a b) -> a c b", b=128), writes=[X])
    fft_stage(P, ps, X, 128, cs1, tw, c2s2, Y, T1, T2, o1, o2, "l")
    Xc = P.sbuf("Xc", [2, CPC, 128])
    P.dma("sp", Xc[:], fc[:].rearrange("c (a b) -> a c b", b=128), writes=[Xc])
    fft_stage(P, ps, Xc, 2, cs1c, twc, c2s2, Y, T1, T2, o1c, o2c, "c")
    P.emit(final_bufs=[o1, o2, o1c, o2c])
    print("LF stats", P.stats())
    return nc


def lf_tables():
    a = np.arange(128)
    ang1 = 2 * np.pi * np.outer(a, a) / 128.0
    sc = 1.0 / np.sqrt(LSEQ * 64.0)
    cs1 = np.concatenate([np.cos(ang1), np.sin(ang1)], axis=1).astype(np.float64) * sc
    angt = 2 * np.pi * np.outer(a, a) / LSEQ
    tw = np.stack([np.cos(angt), np.sin(angt)], axis=1)
    c2s2 = np.stack([np.cos(ang1), np.sin(ang1), -np.sin(ang1)], axis=1)
    b = np.arange(2)
    angc = 2 * np.pi * np.outer(b, b) / 2.0
    scc = 1.0 / np.sqrt(LCTX * 64.0)
    cs1c = np.concatenate([np.cos(angc), np.sin(angc)], axis=1) * scc
    angtc = 2 * np.pi * np.outer(a, b) / LCTX
    twc = np.stack([np.cos(angtc), np.sin(angtc)], axis=1)
    f = lambda x: np.ascontiguousarray(x.astype(np.float32))
    return dict(cs1=f(cs1), tw=f(tw), c2s2=f(c2s2), cs1c=f(cs1c), twc=f(twc))


def run_lf(nc, fT_lat, fT_ctx):
    tb = lf_tables()
    in_maps = []
    for i in range(8):
        m = dict(tb)
        m["fl"] = np.ascontiguousarray(fT_lat[i * CPC:(i + 1) * CPC])
        m["fc"] = np.ascontiguousarray(fT_ctx[i * CPC:(i + 1) * CPC])
        in_maps.append(m)
    res = run_bass_kernel_spmd(nc, in_maps, core_ids=list(range(8))).results
    cat = lambda k: np.concatenate([r[k] for r in res], axis=0)
    return cat("p1"), cat("p2"), cat("p1c"), cat("p2c")


def chan_mats():
    a = np.arange(64)
    ang = 2 * np.pi * np.outer(a, a) / 64.0
    return np.cos(ang).astype(np.float32), np.sin(ang).astype(np.float32)


D = 1024


def build_mod():
    nc = bass.Bass("TRN2", target_bir_lowering=False)
    w = nc.dram_tensor("w", [D, 3072], F32, kind="ExternalInput").ap()
    b = nc.dram_tensor("b", [3072], F32, kind="ExternalInput").ap()
    cc = nc.dram_tensor("cc", [2, D], F32, kind="ExternalInput").ap()
    P = Prog(nc)
    out = P.dram("mod", [128, 24, 2], F32, kind="ExternalOutput")
    ct = P.sbuf("ct", [128, 2, 8], F32)
    st = P.sbuf("st", [128, 8, 2], F32)
    bt = P.sbuf("bt", [128, 24], F32)
    ot = P.sbuf("ot", [128, 24, 2], F32)
    wt = [P.sbuf(f"wt{i}", [128, 8, 512], F32) for i in range(2)]
    ps = [P.psum(f"ps{i}", [128, 512], F32) for i in range(2)]
    P.dma("sp", ct[:], cc.rearrange("j (k p) -> p j k", p=128), writes=[ct], allow_slow_non_contiguous=True)
    P.dma("sp", bt[:], b.rearrange("(m p) -> p m", p=128), writes=[bt], allow_slow_non_contiguous=True)
    for j in range(2):
        P.op("act", lambda e, j=j: e.activation(out=st[:, :, j], in_=ct[:, j, :], func=AF.Silu), reads=[ct], writes=[st])
    for g in range(6):
        wb = wt[g % 2]
        P.dma("sp", wb[:], w[:, g * 512:(g + 1) * 512].rearrange("(k p) m -> p k m", p=128), writes=[wb])
        for mi in range(4):
            m = g * 4 + mi
            pb = ps[m % 2]
            for k in range(8):
                P.op("pe", lambda e, k=k, mi=mi, wb=wb, pb=pb: e.matmul(pb[:, 0:2], lhsT=wb[:, k, mi * 128:(mi + 1) * 128],
                                                                   rhs=st[:, k, :], start=(k == 0), stop=(k == 7)),
                     reads=[wb, st], writes=[pb])
            P.op("dve", lambda e, m=m, pb=pb: e.tensor_scalar(out=ot[:, m, :], in0=pb[:, 0:2], scalar1=bt[:, m:m + 1], scalar2=None,
                                                          op0=ALU.add),
                 reads=[pb, bt], writes=[ot])
    P.dma("sp", out[:], ot[:], reads=[ot], writes=[out])
    P.emit(final_bufs=[out])
    return nc


def run_mod(c, c_ctx, w_ada, b_ada):
    nc = build_mod()
    cc = np.stack([c.reshape(-1), c_ctx.reshape(-1)]).astype(np.float32)
    in_maps = []
    for i in range(8):
        l, hf = i // 2, i % 2
        in_maps.append({"w": np.ascontiguousarray(w_ada[l][:, hf * 3072:(hf + 1) * 3072]),
                        "b": np.ascontiguousarray(b_ada[l][hf * 3072:(hf + 1) * 3072]), "cc": cc})
    res = run_bass_kernel_spmd(nc, in_maps, core_ids=list(range(8)))
    mods = []
    for l in range(4):
        mods.append(np.concatenate([res.results[2 * l]["mod"], res.results[2 * l + 1]["mod"]], axis=1))
    return np.stack(mods)


def build_lb():
    nc = bass.Bass("TRN2", target_bir_lowering=False)
    P = Prog(nc)
    di = lambda name, shape, dt=F32: P.dram(name, shape, dt, kind="ExternalInput")
    d = common_inputs(P)
    cons = scan_const_inputs(P)
    PRE = di("PRE", [7, 3, 2, 128, 256])
    wuq_d = di("wuq", [128, 3, 576]); qn_d = di("qn", [128, 3]); qgain_d = di("qgain", [128, 1]); rott_d = di("rott", [128, 128])
    cos_d = di("cosT", [96, TPC]); sin_d = di("sinT", [96, TPC])
    KT = di("KT", [6, 96, NKEY], BF16); VH = di("VH", [6, 128, NKT, 65], BF16); sel_d = di("sel", [65, 64])
    PCS = di("PCS", [2, 2, 128, NTOK])
    w_d = dict(wba=di("wba", [128, 3, 1024]), wbb=di("wbb", [128, 3, 1024]), wbc=di("wbc", [128, 2, 1024]), wout=di("wout", [128, 8, 1024]))
    cmat_d = di("cmat", [128, 2, 128])
    xm_out = P.dram("xm", [D, NTOK], F32, kind="ExternalOutput")
    U = P.dram("U", [NCH, 128, NTOK], F32)
    Ysc = P.dram("Ysc", [2, 3, 128, NTOK], F32)
    PD = P.dram("PD", [2, 3, 128, NTOK], F32)
    YA = P.dram("YA", [3, 128, NTOK], F32)
    YB = P.dram("YB", [3, 128, NTOK], F32)
    QT = P.dram("QT", [6, 96, NTOK], BF16)
    ps = [P.psum(f"ps{i}", [128, 512], F32) for i in range(8)]
    modt = P.sbuf("modt", [128, 48, 2]); g1t = P.sbuf("g1t", [128, 8]); flt = P.sbuf("flt", [128, NT, 2]); rwc = P.sbuf("rwc_t", [128, 12, 3])
    qn_t = P.sbuf("qn_t", [128, 3]); qgain_t = P.sbuf("qgain_t", [128, 1]); rott = P.sbuf("rott_t", [128, 128])
    ones = P.sbuf("ones", [128, 128]); eps_t = P.sbuf("eps_t", [128, 1])
    for (t, s_) in ((modt, d["mod"]), (g1t, d["g1"]), (flt, d["flags"]), (rwc, d["rwc"]), (qn_t, qn_d), (qgain_t, qgain_d), (rott, rott_d)):
        P.dma("sp", t[:], s_[:], writes=[t])
    P.I("dve", "memset", writes=[ones], ap=ones[:], constant=1.0)
    P.I("dve", "memset", writes=[eps_t], ap=eps_t[:], constant=EPS)
    phase_proj(P, ps, d["xT"], d["cT"], modt, g1t, flt, d["win"], CH_B, U, rwc, ones, eps_t)
    scan_pass2(P, ps, U, cons, PRE, Ysc, PD)
    rwkv_out(P, ps, U, cons, Ysc, PD, YA)
    phase_q(P, ps, U, qn_t, wuq_d, qgain_t, rott, cos_d, sin_d, ones, eps_t, QT)
    phase_attn(P, ps, QT, KT, VH, sel_d, YB)
    phase_merge(P, ps, U, YA, YB, PCS, d["xT"], d["cT"], modt, w_d, cmat_d, xm_out)
    P.emit(final_bufs=[xm_out])
    print("LB stats", P.stats())
    return nc


_PROGS = {}


def get_prog(name):
    if name not in _PROGS:
        _PROGS[name] = dict(mod=build_mod, la=build_la, lf=build_lf, lb=build_lb, lc=build_lc)[name]()
    return _PROGS[name]


def layer_forward(inp, l, x_lat, x_ctx, mod_l):
    g = lambda k: np.asarray(inp[k][l], dtype=np.float32)
    maps = host_common(x_lat, x_ctx, mod_l, g("g_norm1"), g("w_in"), g("rw_conv"))
    resA = run_la(get_prog("la"), maps, inp, l)
    bf = ml_dtypes.bfloat16
    KT = np.zeros((6, 96, NKEY), bf)
    Vall = np.zeros((NKEY, 6, 65), bf)
    KT[:, :, 0:256] = resA[0]["kT"][:, :, TPC:]
    Vall[0:256] = resA[0]["v1"][TPC:]
    fT_lat = np.zeros((256, 16384), np.float32)
    for i in range(8):
        KT[:, :, 256 + i * TPC: 256 + (i + 1) * TPC] = resA[i]["kT"][:, :, :TPC]
        Vall[256 + i * TPC: 256 + (i + 1) * TPC] = resA[i]["v1"][:TPC]
        fT_lat[:, i * TPC:(i + 1) * TPC] = resA[i]["fT"].reshape(256, NTOK)[:, :TPC]
    fT_ctx = np.ascontiguousarray(resA[0]["fT"].reshape(256, NTOK)[:, TPC:])
    VH = np.ascontiguousarray(Vall.reshape(NKT, 128, 6, 65).transpose(2, 1, 0, 3))
    segs = [r["seg"] for r in resA]
    P1, P2, P1c, P2c = run_lf(get_prog("lf"), fT_lat, fT_ctx)
    cosT, sinT, rott = rope_tables()
    Cc, Sc = chan_mats()
    cmat = np.zeros((128, 2, 128), np.float32)
    for b in range(2):
        cmat[b * 64:(b + 1) * 64, 0, b * 64:(b + 1) * 64] = Cc
        cmat[b * 64:(b + 1) * 64, 1, b * 64:(b + 1) * 64] = -Sc
    sel = np.zeros((65, 64), np.float32)
    sel[64, :] = 1.0
    qg = np.zeros((128, 1), np.float32)
    qg[:96, 0] = g("q_gain")
    lay3 = lambda w, nk: np.ascontiguousarray(w.reshape(nk, 128, -1).transpose(1, 0, 2))
    shared = dict(scan_consts_host(inp, l))
    shared.update(dict(wuq=lay3(g("w_uq"), 3), qn=lay128(g("mla_q_norm"), 3), qgain=qg, rott=rott, KT=KT, VH=VH, sel=sel,
                       wba=lay3(g("w_branch_a"), 3), wbb=lay3(g("w_branch_b"), 3), wbc=lay3(g("w_branch_c"), 2), wout=lay3(g("w_out"), 8),
                       cmat=cmat))
    in_maps = []
    for i in range(8):
        m = dict(maps[i])
        m.update(shared)
        m["PRE"] = make_pre(segs, i)
        m["cosT"] = np.ascontiguousarray(cosT[:, i * TPC:(i + 1) * TPC])
        m["sinT"] = np.ascontiguousarray(sinT[:, i * TPC:(i + 1) * TPC])
        pcs = np.zeros((2, 2, 128, NTOK), np.float32)
        for a, (Pl, Pc) in enumerate(((P1, P1c), (P2, P2c))):
            for c in range(2):
                pcs[a, c, :, :TPC] = Pl[c * 128:(c + 1) * 128, i * TPC:(i + 1) * TPC]
                pcs[a, c, :, TPC:] = Pc[c * 128:(c + 1) * 128, :]
        m["PCS"] = pcs
        in_maps.append(m)
    resB = run_bass_kernel_spmd(get_prog("lb"), in_maps, core_ids=list(range(8))).results
    xm_lat = np.ascontiguousarray(np.concatenate([r["xm"][:, :TPC] for r in resB], axis=1).T)
    xm_ctx = np.ascontiguousarray(resB[0]["xm"][:, TPC:].T)
    xo_lat, xo_ctx = run_lc(get_prog("lc"), xm_lat, xm_ctx, mod_l, g("g_norm2"), g("w_up"), g("ffn_conv"), g("w_down"))
    return xo_lat, xo_ctx, xm_lat, xm_ctx


def kernel(**inputs):
    inp = {k: np.asarray(v) for k, v in inputs.items()}
    mods = run_mod(inp["c"], inp["c_ctx"], inp["w_ada"], inp["b_ada"])
    x_lat = np.ascontiguousarray(inp["x"][0], dtype=np.float32)
    x_ctx = np.ascontiguousarray(inp["ctx"][0], dtype=np.float32)
    for l in range(4):
        x_lat, x_ctx, _, _ = layer_forward(inp, l, x_lat, x_ctx, np.ascontiguousarray(mods[l]))
    return x_lat[None].astype(np.float32)
```
